# Optimizing a Trainium2 kernel written in Bass

```python
import math
import jax, jax.numpy as jnp
from jax import lax
import numpy as np

D_MODEL = 1024
BATCH = 32
SEQ = 2048
DEPTH = 1

CTX_LEN = 256
GRID_W = 64
SSM_WIDTH = 512
SSM_GROUP = 16
SSM_GROUPS = SSM_WIDTH // SSM_GROUP
SSM_STATE = 64
DT_MIN = 1e-3
DT_MAX = 1e-1
SSM_C_STD = 0.5
SCONV_WIDTH = 512
SCONV_K = 3
FFN_HIDDEN = 2816
FFN_K = 3
N_BRANCH = 2
PROJ_COLS = SSM_WIDTH + 3 * SCONV_WIDTH + N_BRANCH * D_MODEL
N_MOD = 6
EPS = 1e-6

kernel_name = 'hybrid_s5_shortconv_convffn_prefix_block'


def rmsnorm(x, g):
    xf = x.astype(jnp.float32)
    r = lax.rsqrt(jnp.mean(xf * xf, axis=-1, keepdims=True) + EPS)
    return (xf * r).astype(x.dtype) * g


def modulate(h, shift, scale):
    return h * (1 + scale) + shift


def adaln(cond, w, b):
    m = jax.nn.silu(cond) @ w + b
    return jnp.split(m, N_MOD, axis=-1)


def s5_discretise(lam_re, lam_im, log_dt, b_re, b_im):
    lam = lax.complex(lam_re.astype(jnp.float32), lam_im.astype(jnp.float32))
    dt = jnp.exp(log_dt.astype(jnp.float32))[:, None]
    a_bar = jnp.exp(lam * dt)
    b = lax.complex(b_re.astype(jnp.float32), b_im.astype(jnp.float32))
    b_bar = ((a_bar - 1) / lam)[..., None] * b
    return a_bar, b_bar


def _lin_rec_combine(e1, e2):
    a1, b1 = e1
    a2, b2 = e2
    return a1 * a2, a2 * b1 + b2


def s5_states(u, a_bar, b_bar, init, reverse):
    uf = u.astype(jnp.float32)
    if reverse:
        uf = uf[:, ::-1]
    bu = jnp.einsum('blgh,gph->lbgp', uf.astype(jnp.complex64), b_bar)
    if init is not None:
        bu = bu.at[0].add(a_bar * init)
    a = jnp.broadcast_to(a_bar, (bu.shape[0], 1) + a_bar.shape)
    _, states = lax.associative_scan(_lin_rec_combine, (a, bu), axis=0)
    return states


def s5_readout(states, c_re, c_im, reverse):
    c = lax.complex(c_re.astype(jnp.float32), c_im.astype(jnp.float32))
    y = jnp.real(jnp.einsum('lbgp,ghp->blgh', states, c))
    return y[:, ::-1] if reverse else y


def s5_glu(y, w, b):
    z = jax.nn.gelu(y)
    return z * jax.nn.sigmoid(z @ w + b)


def s5_branch(u_lat, u_ctx, lam_re, lam_im, log_dt, b_re, b_im, c_re, c_im, d_skip, glu_w, glu_b, ctx_out):
    bsz, seq, _ = u_lat.shape
    ctx_len = u_ctx.shape[1]
    ul = u_lat.reshape(bsz, seq, SSM_GROUPS, SSM_GROUP)
    uc = u_ctx.reshape(bsz, ctx_len, SSM_GROUPS, SSM_GROUP)
    y_lat = d_skip * u_lat.astype(jnp.float32)
    y_ctx = d_skip * u_ctx.astype(jnp.float32) if ctx_out else None
    for direction in range(2):
        rev = direction == 1
        a_bar, b_bar = s5_discretise(lam_re[direction], lam_im[direction], log_dt[direction],
                                     b_re[direction], b_im[direction])
        s_ctx = s5_states(uc, a_bar, b_bar, None, rev)
        s_lat = s5_states(ul, a_bar, b_bar, s_ctx[-1], rev)
        y_lat = y_lat + s5_readout(s_lat, c_re[direction], c_im[direction], rev).reshape(bsz, seq, SSM_WIDTH)
        if ctx_out:
            y_ctx = y_ctx + s5_readout(s_ctx, c_re[direction], c_im[direction], rev).reshape(bsz, ctx_len, SSM_WIDTH)
    y_lat = s5_glu(y_lat, glu_w, glu_b).astype(u_lat.dtype)
    if ctx_out:
        y_ctx = s5_glu(y_ctx, glu_w, glu_b).astype(u_ctx.dtype)
    return y_lat, y_ctx


def depthwise_conv1d(x, w):
    return lax.conv_general_dilated(x, w[:, None, :], window_strides=(1,), padding='SAME',
                                    dimension_numbers=('NWC', 'WIO', 'NWC'),
                                    feature_group_count=x.shape[-1])


def depthwise_conv2d(x, w, rows, cols):
    bsz, seq, ch = x.shape
    y = lax.conv_general_dilated(x.reshape(bsz, rows, cols, ch), w[:, :, None, :],
                                 window_strides=(1, 1), padding='SAME',
                                 dimension_numbers=('NHWC', 'HWIO', 'NHWC'),
                                 feature_group_count=ch)
    return y.reshape(bsz, seq, ch)


def mixer_merge(p, y_a, sconv_w, proj_a, proj_b, w_out):
    o = SSM_WIDTH
    w = SCONV_WIDTH
    b_gate = p[..., o:o + w]
    c_gate = p[..., o + w:o + 2 * w]
    x_val = p[..., o + 2 * w:o + 3 * w]
    o2 = o + 3 * w
    gate_a = jax.nn.sigmoid(p[..., o2:o2 + D_MODEL])
    gate_b = jax.nn.sigmoid(p[..., o2 + D_MODEL:o2 + 2 * D_MODEL])
    y_b = b_gate * depthwise_conv1d(c_gate * x_val, sconv_w)
    merged = gate_a * (y_a @ proj_a) + gate_b * (y_b @ proj_b)
    return merged @ w_out


def conv_ffn(h, rows, cols, w_up, conv_w, w_down):
    u = depthwise_conv2d(h @ w_up, conv_w, rows, cols)
    a, v = jnp.split(u, 2, axis=-1)
    return (jax.nn.silu(a) * v) @ w_down


def setup_inputs(seed: int = 0) -> dict:
    key = jax.random.key(seed)
    ks = jax.random.split(key, 32)
    f32 = jnp.float32
    G, P, H = SSM_GROUPS, SSM_STATE, SSM_GROUP
    n_idx = jnp.arange(P, dtype=f32)
    lam_re = -0.5 + 0.01 * jax.random.normal(ks[8], (DEPTH, 2, G, P), f32)
    lam_im = math.pi * n_idx + 0.01 * jax.random.normal(ks[9], (DEPTH, 2, G, P), f32)
    log_dt = jax.random.uniform(ks[10], (DEPTH, 2, G), f32, math.log(DT_MIN), math.log(DT_MAX))
    return {
        'x': jax.random.normal(ks[0], (BATCH, SEQ, D_MODEL), f32),
        'c': jax.random.normal(ks[1], (BATCH, D_MODEL), f32),
        'ctx': jax.random.normal(ks[2], (BATCH, CTX_LEN, D_MODEL), f32),
        'c_ctx': jax.random.normal(ks[3], (D_MODEL,), f32),
        'mod_w': jax.random.normal(ks[4], (DEPTH, D_MODEL, N_MOD * D_MODEL), f32) * 0.5 * D_MODEL ** -0.5,
        'mod_b': 0.01 * jax.random.normal(ks[5], (DEPTH, N_MOD * D_MODEL), f32),
        'norm1_g': 1.0 + 0.02 * jax.random.normal(ks[6], (DEPTH, D_MODEL), f32),
        'norm2_g': 1.0 + 0.02 * jax.random.normal(ks[7], (DEPTH, D_MODEL), f32),
        'w_in': jax.random.normal(ks[11], (DEPTH, D_MODEL, PROJ_COLS), f32) * D_MODEL ** -0.5,
        'ssm_lambda_re': lam_re,
        'ssm_lambda_im': lam_im,
        'ssm_log_dt': log_dt,
        'ssm_b_re': jax.random.normal(ks[12], (DEPTH, 2, G, P, H), f32) * (2 * H) ** -0.5,
        'ssm_b_im': jax.random.normal(ks[13], (DEPTH, 2, G, P, H), f32) * (2 * H) ** -0.5,
        'ssm_c_re': jax.random.normal(ks[14], (DEPTH, 2, G, H, P), f32) * SSM_C_STD,
        'ssm_c_im': jax.random.normal(ks[15], (DEPTH, 2, G, H, P), f32) * SSM_C_STD,
        'ssm_d': jax.random.normal(ks[16], (DEPTH, SSM_WIDTH), f32),
        'ssm_glu_w': jax.random.normal(ks[17], (DEPTH, SSM_WIDTH, SSM_WIDTH), f32) * SSM_WIDTH ** -0.5,
        'ssm_glu_b': 0.01 * jax.random.normal(ks[18], (DEPTH, SSM_WIDTH), f32),
        'sconv_w': jax.random.normal(ks[19], (DEPTH, SCONV_K, SCONV_WIDTH), f32) * SCONV_K ** -0.5,
        'proj_a': jax.random.normal(ks[20], (DEPTH, SSM_WIDTH, D_MODEL), f32) * SSM_WIDTH ** -0.5,
        'proj_b': jax.random.normal(ks[21], (DEPTH, SCONV_WIDTH, D_MODEL), f32) * SCONV_WIDTH ** -0.5,
        'w_out': jax.random.normal(ks[22], (DEPTH, D_MODEL, D_MODEL), f32) * D_MODEL ** -0.5,
        'ffn_w_up': jax.random.normal(ks[23], (DEPTH, D_MODEL, 2 * FFN_HIDDEN), f32) * D_MODEL ** -0.5,
        'ffn_conv_w': jax.random.normal(ks[24], (DEPTH, FFN_K, FFN_K, 2 * FFN_HIDDEN), f32) / FFN_K,
        'ffn_w_down': jax.random.normal(ks[25], (DEPTH, FFN_HIDDEN, D_MODEL), f32) * FFN_HIDDEN ** -0.5,
        'final_g': 1.0 + 0.02 * jax.random.normal(ks[26], (D_MODEL,), f32),
    }


def reference(x, c, ctx, c_ctx, mod_w, mod_b, norm1_g, norm2_g, w_in, ssm_lambda_re, ssm_lambda_im,
              ssm_log_dt, ssm_b_re, ssm_b_im, ssm_c_re, ssm_c_im, ssm_d, ssm_glu_w, ssm_glu_b,
              sconv_w, proj_a, proj_b, w_out, ffn_w_up, ffn_conv_w, ffn_w_down, final_g):
    rows = x.shape[1] // GRID_W
    ctx_len = ctx.shape[1]
    for i in range(DEPTH):
        last = i == DEPTH - 1
        sh1, sc1, g1, sh2, sc2, g2 = adaln(c[:, None, :], mod_w[i], mod_b[i])
        csh1, csc1, cg1, csh2, csc2, cg2 = adaln(c_ctx[None, None, :], mod_w[i], mod_b[i])

        h = modulate(rmsnorm(x, norm1_g[i]), sh1, sc1)
        hc = modulate(rmsnorm(ctx, norm1_g[i]), csh1, csc1)
        p = h @ w_in[i]
        pc = hc @ (w_in[i][:, :SSM_WIDTH] if last else w_in[i])
        y_a, y_a_ctx = s5_branch(p[..., :SSM_WIDTH], pc[..., :SSM_WIDTH],
                                 ssm_lambda_re[i], ssm_lambda_im[i], ssm_log_dt[i],
                                 ssm_b_re[i], ssm_b_im[i], ssm_c_re[i], ssm_c_im[i],
                                 ssm_d[i], ssm_glu_w[i], ssm_glu_b[i], not last)
        x = x + g1 * mixer_merge(p, y_a, sconv_w[i], proj_a[i], proj_b[i], w_out[i])
        if not last:
            ctx = ctx + cg1 * mixer_merge(pc, y_a_ctx, sconv_w[i], proj_a[i], proj_b[i], w_out[i])

        h2 = modulate(rmsnorm(x, norm2_g[i]), sh2, sc2)
        x = x + g2 * conv_ffn(h2, rows, GRID_W, ffn_w_up[i], ffn_conv_w[i], ffn_w_down[i])
        if not last:
            hc2 = modulate(rmsnorm(ctx, norm2_g[i]), csh2, csc2)
            ctx = ctx + cg2 * conv_ffn(hc2, 1, ctx_len, ffn_w_up[i], ffn_conv_w[i], ffn_w_down[i])
    return rmsnorm(x, final_g)
```

```python
import os
import math
import numpy as np
import concourse.bass as bass
import concourse.mybir as mybir
from concourse.bass_utils import run_bass_kernel_spmd

F32 = mybir.dt.float32
BF16 = mybir.dt.bfloat16
AF = mybir.ActivationFunctionType
ALU = mybir.AluOpType

D = 1024
SEQ = 2048
CTXL = 256
NTOK = SEQ + CTXL
NCH = NTOK // 8
FFH = 2816
NPAIR = 22
EPS = 1e-6
ENG = ['pe', 'act', 'dve', 'pool', 'sp']

O_N1G, O_N2G, O_GLUB, O_SCONV, O_CONVW, O_MODB, O_DSH, O_LRE, O_LIM, O_LDT = 0, 8, 16, 20, 32, 428, 476, 508, 540, 572
NSM = 604
O_ID, O_MF, O_MR, O_NV = 0, 128, 256, 384
NCT = 896


class Slot:
    def __init__(self, sem):
        self.sem = sem
        self.count = 0


class Tracker:
    def __init__(self):
        self.ops = {e: [] for e in ENG}
        self.lastw = {}
        self.readers = {}
        self.observed = {e: set() for e in ENG}
        self.last_compute = {e: None for e in ENG}
        self.outstanding = []

    def _deps(self, eng, r, w):
        toks = []
        for k in r:
            t = self.lastw.get(k)
            if t is not None:
                toks.append(t)
        for k in w:
            t = self.lastw.get(k)
            if t is not None:
                toks.append(t)
            toks.extend(self.readers.get(k, ()))
        best = {}
        for t in toks:
            if t[0] == 'E':
                if t[1] == eng and eng == 'pe':
                    continue
                key = ('E', t[1])
            else:
                key = ('D', id(t[1]))
            if key not in best or best[key][2] < t[2]:
                best[key] = t
        out = list(best.values())
        for t in out:
            if t[0] == 'E':
                self.observed[t[1]].add(t[2])
        return out

    def _commit(self, tok, r, w):
        for k in w:
            self.lastw[k] = tok
            self.readers[k] = []
        for k in r:
            if k not in w:
                self.readers.setdefault(k, []).append(tok)

    def op(self, eng, fn, r=(), w=()):
        waits = self._deps(eng, r, w)
        idx = len(self.ops[eng])
        self.ops[eng].append(dict(kind='op', fn=fn, waits=waits))
        self.last_compute[eng] = idx
        self._commit(('E', eng, idx), r, w)

    def dma(self, eng, slot, out, in_, r=(), w=()):
        waits = self._deps(eng, r, w)
        if slot.count > 0 and not any(t[0] == 'D' and t[1] is slot for t in waits):
            waits.append(('D', slot, slot.count))
        slot.count += 16
        self.ops[eng].append(dict(kind='dma', out=out, in_=in_, slot=slot, waits=waits))
        tok = ('D', slot, slot.count)
        self.outstanding.append(tok)
        self._commit(tok, r, w)

    def fence(self):
        toks = []
        for e in ENG:
            if self.last_compute[e] is not None:
                toks.append(('E', e, self.last_compute[e]))
        best = {}
        for t in self.outstanding:
            k = id(t[1])
            if k not in best or best[k][2] < t[2]:
                best[k] = t
        toks.extend(best.values())
        for e in ENG:
            waits = [t for t in toks if not (t[0] == 'E' and t[1] == e and e in ('pe', 'sp'))]
            for t in waits:
                if t[0] == 'E':
                    self.observed[t[1]].add(t[2])
            self.ops[e].append(dict(kind='nop', waits=waits))
        self.lastw = {}
        self.readers = {}
        self.outstanding = []

    def emit(self, eng, e, sems):
        ranks = {}
        for en in ENG:
            ranks[en] = {idx: i + 1 for i, idx in enumerate(sorted(self.observed[en]))}
        have = {}
        myrank = ranks[eng]
        for idx, o in enumerate(self.ops[eng]):
            for t in o['waits']:
                if t[0] == 'E':
                    sem = sems[t[1]]
                    val = ranks[t[1]][t[2]]
                    hk = t[1]
                else:
                    sem = t[1].sem
                    val = t[2]
                    hk = id(t[1])
                if have.get(hk, 0) >= val:
                    continue
                e.wait_ge(sem, val)
                have[hk] = val
            if o['kind'] == 'op':
                ins = o['fn'](e)
                if idx in myrank:
                    ins.then_inc(sems[eng], 1)
            elif o['kind'] == 'dma':
                e.dma_start(out=o['out'], in_=o['in_']).then_inc(o['slot'].sem, 16)


def _rs(ap, pat, **kw):
    return ap.rearrange(pat, **kw)


class StopBuild(Exception):
    pass


def build_program(NB, debug=False, stop=None):
    def ck(i):
        if stop is not None and i == stop:
            raise StopBuild()
    nc = bass.Bass("TRN2", target_bir_lowering=False)
    dt_in = lambda name, shape: nc.dram_tensor(name, shape, F32, kind="ExternalInput").ap()
    x_d = dt_in("x", [NB, SEQ, D])
    ctx_d = dt_in("ctx", [NB, CTXL, D])
    cT_d = dt_in("cT", [128, 8, 5])
    modw_d = dt_in("mod_w", [D, 6 * D])
    modbrow_d = dt_in("modb_row", [1, 6 * D])
    fgrow_d = dt_in("fg_row", [1, D])
    smalls_d = dt_in("smalls", [128, NSM])
    consts_d = dt_in("consts", [128, NCT])
    ssmbc_d = dt_in("ssmbc", [128, 4, 32, 16])
    win_d = dt_in("w_in", [D, 4096])
    gluw_d = dt_in("glu_w", [512, 512])
    pa_d = dt_in("proj_a", [512, D])
    pb_d = dt_in("proj_b", [512, D])
    wo_d = dt_in("w_out", [D, D])
    wup_d = dt_in("w_up", [D, 2 * FFH])
    wdn_d = dt_in("w_down", [FFH, D])
    out_d = nc.dram_tensor("out", [NB, SEQ, D], F32, kind="ExternalOutput").ap()
    ssmc_d = nc.dram_tensor("ssmc_scr", [128, 28672], BF16).ap()
    gsc_d = nc.dram_tensor("gsc_scr", [NB * 2, 128, D], F32).ap()
    dbg = {}

    T = Tracker()
    import contextlib
    es = contextlib.ExitStack()
    with es:
        sb = lambda name, shape, dt: es.enter_context(nc.sbuf_tensor(name, shape, dt))
        sems = {e: es.enter_context(nc.semaphore("s_" + e)) for e in ENG}
        nslot = [0]

        def newslot():
            nslot[0] += 1
            return Slot(es.enter_context(nc.semaphore("d%d" % nslot[0])))

        SMk = sb("smk", [128, NSM], F32)
        IDB = sb("idb", [128, 128], BF16)
        MODS = sb("mods", [128, 4, 8, 5], F32)
        S1 = sb("s1", [128, 8, 5], F32)
        S2 = sb("s2", [128, 8, 5], F32)
        SCB = sb("scb", [128, 8, 5], BF16)
        FG = sb("fg", [128, D], F32)
        ATAB = sb("atab", [128, 2, 128], F32)
        HT = sb("ht", [128, 8, NTOK], BF16)
        BIGB = 116 * 1024
        BIG = sb("big", [128, BIGB // 2], BF16)
        WS = [sb("ws%d" % i, [128, 4096], BF16) for i in range(2)]
        SMT = [sb("smt%d" % i, [128, 512], F32) for i in range(4)]
        XT = [sb("xt%d" % i, [128, D], F32) for i in range(3)]
        XN = [sb("xn%d" % i, [128, D], BF16) for i in range(2)]
        GT = sb("gt", [128, D], F32)
        STAT = sb("stat", [128, 8], F32)
        DGS = sb("dgs", [128, 12, 128], BF16)
        EPST = sb("epst", [128, 1], F32)
        PB = [es.enter_context(nc.psum_tensor("pb%d" % i, [128, 2048], F32)) for i in range(2)]

        def bank(i):
            return PB[i // 4][:, (i % 4) * 512:(i % 4) * 512 + 512], "ps%d" % i

        def bankb(i):
            return PB[i // 4][:, (i % 4) * 512:(i % 4) * 512 + 512].bitcast(BF16), "ps%d" % i

        def big(off, dtype, n):
            esz = 4 if dtype == F32 else 2
            assert off % 4 == 0 and off + n * esz <= BIGB, (off, n)
            a = BIG[:, off // 2: off // 2 + n * esz // 2]
            if dtype == F32:
                a = a.bitcast(F32)
            return a

        KB = 1024
        ws_slots = [newslot() for _ in range(2)]
        ws_ctr = [0]

        def wload(src_ap, ncols_view, key_extra=None):
            i = ws_ctr[0] % 2
            ws_ctr[0] += 1
            return i

        def cast_dma(slot, dst, src, wkeys):
            T.dma('pool', slot, dst, src, r=(), w=wkeys)

        def dump(nm, ap, shp, dt_, rk):
            dd = nc.dram_tensor("dbg_" + nm, shp, dt_, kind="ExternalOutput").ap()
            T.dma('sp', newslot(), dd[:, :], ap, r=rk, w=['dbg_' + nm])
            dbg[nm] = dd

        try:
          if True:
            pass

            ld = newslot()
            CTb = big(0, F32, NCT)
            SBC = big(4 * KB, F32, 2048)
            T.dma('sp', ld, SMk[:], smalls_d[:, :], w=['smk'])
            T.dma('sp', newslot(), CTb, consts_d[:, :], w=['ct'])
            T.dma('sp', newslot(), SBC, ssmbc_d.rearrange("p a g h -> p (a g h)"), w=['sbc'])
            CTt = XT[0][:, 0:40]
            T.dma('sp', newslot(), XT[0][:, 0:40], cT_d.rearrange("p k j -> p (k j)"), w=['xt0'])
            T.dma('sp', newslot(), FG[:], fgrow_d[0:1, :].partition_broadcast(128), w=['fg'])
            T.op('dve', lambda e: e.tensor_copy(out=IDB[:], in_=CTb[:, O_ID:O_ID + 128]), r=['ct'], w=['idb'])
            T.op('act', lambda e: e.activation(out=SCB[:].rearrange("p k j -> p (k j)"), in_=XT[0][:, 0:40], func=AF.Silu),
                 r=['xt0'], w=['scb'])

            ck(1)
            SCREP = big(16 * KB, BF16, NB * 8 * 128)
            for j in range(NB):
                T.op('dve', lambda e, j=j: e.tensor_copy(
                    out=SCREP[:, j * 1024:(j + 1) * 1024].rearrange("p (k m) -> p k m", m=128),
                    in_=SCB[:, :, j:j + 1].to_broadcast([128, 8, 128])), r=['scb'], w=['screp%d' % j])
            MBR = big(32 * KB, F32, 1024)
            fmk = {0: 0, 1: 1, 3: 2, 4: 3}
            pm, pmk = bank(0)
            for kind in range(6):
                for half in range(2):
                    wi = (kind * 2 + half) % 2
                    c0 = kind * D + half * 512
                    cast_dma(ws_slots[wi], WS[wi][:].rearrange("p (k n) -> p k n", n=512),
                             modw_d[:, c0:c0 + 512].rearrange("(k p) n -> p k n", p=128), ['ws%d' % wi])
                    Wv = WS[wi][:].rearrange("p (k n) -> p k n", n=512)
                    if kind in fmk:
                        for tl in range(4):
                            t8 = half * 4 + tl
                            col = (fmk[kind] * 8 + t8) * 5
                            for k in range(8):
                                T.op('pe', lambda e, Wv=Wv, k=k, tl=tl, col=col: e.matmul(
                                    pm[:, col:col + 5], lhsT=Wv[:, k, tl * 128:(tl + 1) * 128], rhs=SCB[:, k, :],
                                    start=(k == 0), stop=(k == 7)), r=['ws%d' % wi, 'scb'], w=[pmk])
                    else:
                        gi = 0 if kind == 2 else 1
                        if half == 0:
                            T.dma('sp', ld, MBR, modbrow_d[0:1, kind * D:(kind + 1) * D].partition_broadcast(128), w=['mbr'])
                        for j in range(NB):
                            pg, pgk = bank(2 + (j % 2))
                            for k in range(8):
                                T.op('pe', lambda e, Wv=Wv, k=k, j=j, pg=pg: e.matmul(
                                    pg[:, :], lhsT=SCREP[:, j * 1024 + k * 128: j * 1024 + (k + 1) * 128], rhs=Wv[:, k, :],
                                    start=(k == 0), stop=(k == 7)), r=['ws%d' % wi, 'screp%d' % j], w=[pgk])
                            T.op('dve', lambda e, pg=pg, half=half: e.tensor_tensor(
                                out=GT[:, half * 512:(half + 1) * 512], in0=pg[:, :], in1=MBR[:, half * 512:(half + 1) * 512],
                                op=ALU.add), r=[pgk, 'mbr'], w=['gt%d' % half])
                            T.dma('sp', ld, gsc_d[j * 2 + gi, :, half * 512:(half + 1) * 512], GT[:, half * 512:(half + 1) * 512],
                                  r=['gt%d' % half], w=['gsc%d_%d_%d' % (j, gi, half)])
            ck(2)
            for kind, f in fmk.items():
                T.op('dve', lambda e, kind=kind, f=f: e.tensor_tensor(
                    out=MODS[:, f, :, :], in0=pm[:, f * 40:(f + 1) * 40].rearrange("p (k j) -> p k j", j=5),
                    in1=SMk[:, O_MODB + kind * 8:O_MODB + kind * 8 + 8].unsqueeze(2).to_broadcast([128, 8, 5]),
                    op=ALU.add), r=[pmk, 'smk'], w=['mods'])
            for (S, f, og) in ((S1, 1, O_N1G), (S2, 3, O_N2G)):
                T.op('dve', lambda e, S=S, f=f, og=og: e.scalar_tensor_tensor(
                    out=S[:], in0=MODS[:, f, :, :], scalar=1.0,
                    in1=SMk[:, og:og + 8].unsqueeze(2).to_broadcast([128, 8, 5]),
                    op0=ALU.add, op1=ALU.mult), r=['mods', 'smk'], w=['s12'])

            T.op('dve', lambda e: e.memset(EPST[:], EPS), w=['epst'])
            for q_ in range(12):
                T.op('dve', lambda e, q_=q_: e.tensor_scalar(out=DGS[:, q_, :], in0=IDB[:], scalar1=SMk[:, O_SCONV + q_:O_SCONV + q_ + 1],
                                                          scalar2=None, op0=ALU.mult), r=['idb', 'smk'], w=['dgs'])
            ck(3)
            o = 36 * KB
            def alloc(n, dtype=F32):
                nonlocal o
                v = big(o, dtype, n)
                o += n * (4 if dtype == F32 else 2)
                return v
            HTF = HT[:].rearrange("p k n -> p (k n)")
            oh = 0
            def alloch(n, dtype=F32):
                nonlocal oh
                esz = 4 if dtype == F32 else 2
                a = HTF[:, oh // 2: oh // 2 + n * esz // 2]
                if dtype == F32:
                    a = a.bitcast(F32)
                oh += n * esz
                assert oh <= 36 * KB
                return a
            DT = alloch(32); LRD = alloch(32); TH = alloch(32)
            ANG = alloch(512); MAG = alloch(512); TMPA = alloch(512)
            SN = alloch(512); CS = alloch(512)
            LMAG = MAG; PWR = CS; PWI = SN
            AR1 = alloch(32); NRE = alloch(32); NIM = alloch(32); DEN = alloch(32); TQ = alloch(32); KR = alloch(32); KI = alloch(32)
            KBR = alloch(512); KBI = alloch(512); TK = alloch(512)
            NEGPI = alloch(2)
            lre = SMk[:, O_LRE:O_LRE + 32]; lim = SMk[:, O_LIM:O_LIM + 32]; ldt = SMk[:, O_LDT:O_LDT + 32]
            nv = CTb[:, O_NV:O_NV + 512]
            V = 'dve'
            def tt(out, a, b, op, r, w):
                T.op(V, lambda e: e.tensor_tensor(out=out, in0=a, in1=b, op=op), r=r, w=w)
            T.op('dve', lambda e: e.memset(NEGPI, -math.pi), w=['negpi'])
            T.op('act', lambda e: e.activation(out=DT, in_=ldt, func=AF.Exp), r=['smk'], w=['dt'])
            tt(LRD, lre, DT, ALU.mult, ['smk', 'dt'], ['lrd'])
            tt(TH, lim, DT, ALU.mult, ['smk', 'dt'], ['th'])
            b3 = lambda a: a.unsqueeze(1).to_broadcast([128, 16, 32])
            r3 = lambda a: a.rearrange("p (n g) -> p n g", g=32)
            tt(r3(ANG), r3(nv), b3(TH), ALU.mult, ['ct', 'th'], ['ang'])
            tt(r3(LMAG), r3(nv), b3(LRD), ALU.mult, ['ct', 'lrd'], ['lmag'])
            T.op('act', lambda e: e.activation(out=MAG, in_=LMAG, func=AF.Exp), r=['lmag'], w=['mag', 'lmag'])
            TWO_PI = 2.0 * math.pi
            I32 = mybir.dt.int32
            XP = XT[1][:, 0:512]; TF = XT[1][:, 512:1024]; TI = SMT[2][:, :].bitcast(I32)

            def sincos(dst, shift, wk):
                T.op('dve', lambda e: e.tensor_scalar(out=XP, in0=ANG, scalar1=shift + 64.0 * TWO_PI, scalar2=None, op0=ALU.add),
                     r=['ang'], w=['xp'])
                T.op('dve', lambda e: e.tensor_scalar(out=TF, in0=XP, scalar1=1.0 / TWO_PI, scalar2=None, op0=ALU.mult), r=['xp'], w=['tf'])
                T.op('dve', lambda e: e.tensor_copy(out=TI, in_=TF), r=['tf'], w=['ti'])
                T.op('dve', lambda e: e.tensor_copy(out=TF, in_=TI), r=['ti'], w=['tf'])
                T.op('dve', lambda e: e.scalar_tensor_tensor(out=TMPA, in0=TF, scalar=-TWO_PI, in1=XP, op0=ALU.mult, op1=ALU.add),
                     r=['tf', 'xp'], w=['tmpa'])
                T.op('dve', lambda e: e.tensor_scalar(out=TF, in0=TMPA, scalar1=math.pi, scalar2=None, op0=ALU.is_ge), r=['tmpa'], w=['tf'])
                T.op('dve', lambda e: e.scalar_tensor_tensor(out=TMPA, in0=TF, scalar=-TWO_PI, in1=TMPA, op0=ALU.mult, op1=ALU.add),
                     r=['tf', 'tmpa'], w=['tmpa'])
                T.op('dve', lambda e: e.tensor_scalar(out=TMPA, in0=TMPA, scalar1=-math.pi, scalar2=math.pi, op0=ALU.max, op1=ALU.min),
                     r=['tmpa'], w=['tmpa'])
                T.op('act', lambda e: e.activation(out=dst, in_=TMPA, func=AF.Sin), r=['tmpa'], w=[wk])
            sincos(SN, 0.0, 'sn')
            sincos(CS, math.pi / 2, 'cs')
            tt(PWR, MAG, CS, ALU.mult, ['mag', 'cs'], ['pwr', 'cs'])
            tt(PWI, MAG, SN, ALU.mult, ['mag', 'sn'], ['pwi', 'sn'])
            pw = lambda P, n: P[:, n * 32:(n + 1) * 32]
            a_r = pw(PWR, 8); a_i = pw(PWI, 8)
            T.op('dve', lambda e: e.tensor_scalar(out=AR1, in0=a_r, scalar1=-1.0, scalar2=None, op0=ALU.add), r=['pwr'], w=['ar1'])
            tt(NRE, AR1, lre, ALU.mult, ['ar1', 'smk'], ['nre'])
            tt(TQ, a_i, lim, ALU.mult, ['pwi', 'smk'], ['tq'])
            tt(NRE, NRE, TQ, ALU.add, ['nre', 'tq'], ['nre'])
            tt(NIM, a_i, lre, ALU.mult, ['pwi', 'smk'], ['nim'])
            tt(TQ, AR1, lim, ALU.mult, ['ar1', 'smk', 'nre'], ['tq'])
            tt(NIM, NIM, TQ, ALU.subtract, ['nim', 'tq'], ['nim'])
            tt(DEN, lre, lre, ALU.mult, ['smk'], ['den'])
            tt(TQ, lim, lim, ALU.mult, ['smk', 'nim'], ['tq'])
            tt(DEN, DEN, TQ, ALU.add, ['den', 'tq'], ['den'])
            T.op('dve', lambda e: e.reciprocal(out=DEN, in_=DEN), r=['den'], w=['den'])
            tt(KR, NRE, DEN, ALU.mult, ['nre', 'den'], ['kr'])
            tt(KI, NIM, DEN, ALU.mult, ['nim', 'den'], ['ki'])
            g3 = lambda a: a.rearrange("p (g h) -> p g h", h=16)
            bh = lambda a: a.unsqueeze(2).to_broadcast([128, 32, 16])
            BRE = SBC[:, 0:512]; BIM = SBC[:, 512:1024]; CRE = SBC[:, 1024:1536]; CIM = SBC[:, 1536:2048]
            tt(g3(KBR), g3(BRE), bh(KR), ALU.mult, ['sbc', 'kr'], ['kbr'])
            tt(g3(TK), g3(BIM), bh(KI), ALU.mult, ['sbc', 'ki'], ['tk'])
            tt(KBR, KBR, TK, ALU.subtract, ['kbr', 'tk'], ['kbr'])
            tt(g3(KBI), g3(BIM), bh(KR), ALU.mult, ['sbc', 'kr'], ['kbi'])
            tt(g3(TK), g3(BRE), bh(KI), ALU.mult, ['sbc', 'ki', 'kbr'], ['tk'])
            tt(KBI, KBI, TK, ALU.add, ['kbi', 'tk'], ['kbi'])
            A8r = pw(PWR, 15); A8i = pw(PWI, 15)
            A16r = alloch(32); A16i = alloch(32); TQ2 = alloch(32)
            tt(A16r, A8r, A8r, ALU.mult, ['pwr'], ['a16r'])
            tt(TQ2, A8i, A8i, ALU.mult, ['pwi'], ['tq2'])
            tt(A16r, A16r, TQ2, ALU.subtract, ['a16r', 'tq2'], ['a16r'])
            tt(A16i, A8r, A8i, ALU.mult, ['pwr', 'pwi'], ['a16i'])
            T.op('dve', lambda e: e.tensor_scalar(out=A16i, in0=A16i, scalar1=2.0, scalar2=None, op0=ALU.mult), r=['a16i'], w=['a16i'])
            for q_ in range(4):
                T.op('dve', lambda e, q_=q_: e.tensor_copy(out=ATAB[:, 0, q_ * 32:(q_ + 1) * 32], in_=A16r), r=['a16r'], w=['atab'])
                if q_ % 2 == 0:
                    T.op('dve', lambda e, q_=q_: e.tensor_scalar(out=ATAB[:, 1, q_ * 32:(q_ + 1) * 32], in0=A16i, scalar1=-1.0, scalar2=None, op0=ALU.mult),
                         r=['a16i'], w=['atab'])
                else:
                    T.op('dve', lambda e, q_=q_: e.tensor_copy(out=ATAB[:, 1, q_ * 32:(q_ + 1) * 32], in_=A16i), r=['a16i'], w=['atab'])
            ck(4)
            PR = alloc(4096); PI_ = alloc(4096); TB = alloc(4096)
            L_r = alloc(4096, BF16); L_i = alloc(4096, BF16); R_r = alloc(4096, BF16); R_ni = alloc(4096, BF16)
            BZr = alloch(4096, BF16); BZi = alloch(4096, BF16)
            SSMC = big(0, BF16, 1)
            OFF_UT, OFF_XB, OFF_RG, OFF_SSMC, OFF_ZU = 0, 18 * KB, 50 * KB, 59 * KB, 100 * KB
            g4 = lambda a: a.rearrange("p (g s h) -> p g s h", s=8, h=16)

            SELR = SMT[3][:, 0:256]; SELI = SMT[3][:, 256:512]

            def cprod(Xr, Xi, selF, selR, rk):
                for half, (asc, n0) in enumerate((selF, selR)):
                    hs = slice(half * 64, (half + 1) * 64)
                    for (P, SEL) in ((PWR, SELR), (PWI, SELI)):
                        v = r3(P)[hs]
                        v = v[:, n0:n0 + 8, :] if asc else v[:, n0:(n0 - 8 if n0 - 8 >= 0 else None):-1, :]
                        T.op('act', lambda e, v=v, SEL=SEL, hs=hs: e.activation(out=SEL[hs].rearrange("p (s g) -> p s g", g=32), in_=v, func=AF.Copy),
                             r=['pwr', 'pwi', 'sel'], w=['sel'])
                selv = lambda SEL: SEL.rearrange("p (s g) -> p g s", g=32).unsqueeze(3).to_broadcast([128, 32, 8, 16])
                xr = g3(Xr).unsqueeze(2).to_broadcast([128, 32, 8, 16])
                xi = g3(Xi).unsqueeze(2).to_broadcast([128, 32, 8, 16])
                pr = selv(SELR); pi = selv(SELI)
                tt(g4(PR), xr, pr, ALU.mult, rk + ['sel'], ['prh0', 'prh1'])
                tt(g4(TB), xi, pi, ALU.mult, rk + ['sel'], ['tbh0', 'tbh1'])
                tt(PR, PR, TB, ALU.subtract, ['prh0', 'prh1', 'tbh0', 'tbh1'], ['prh0', 'prh1'])
                tt(g4(PI_), xr, pi, ALU.mult, rk + ['sel'], ['pih0', 'pih1'])
                tt(g4(TB), xi, pr, ALU.mult, rk + ['sel', 'prh0'], ['tbh0', 'tbh1'])
                tt(PI_, PI_, TB, ALU.add, ['pih0', 'pih1', 'tbh0', 'tbh1'], ['pih0', 'pih1'])

            PRK = ['prh0', 'prh1']; PIK = ['pih0', 'pih1']
            rkb = ['kbr', 'kbi', 'pwr', 'pwi']
            rkc = ['sbc', 'pwr', 'pwi']
            cpy = lambda out, in_, r, w: T.op('act', lambda e: e.activation(out=out, in_=in_, func=AF.Copy), r=r, w=w)
            neg = lambda out, in_, r, w: T.op('act', lambda e: e.activation(out=out, in_=in_, func=AF.Copy, scale=-1.0), r=r, w=w)
            cprod(KBR, KBI, (False, 14), (True, 7), rkb)
            cpy(BZr, PR, PRK, ['bzr']); cpy(BZi, PI_, PIK, ['bzi'])
            for c, BZ in enumerate((BZr, BZi)):
                for q in range(4):
                    pt, ptk = bankb(4 + (c * 4 + q) % 2)
                    for gi in range(8):
                        g = q * 8 + gi
                        T.op('pe', lambda e, BZ=BZ, g=g, gi=gi, pt=pt: e.transpose(
                            pt[:, gi * 128:(gi + 1) * 128], BZ[:, g * 128:(g + 1) * 128], IDB[:]),
                            r=['bzr', 'bzi', 'idb'], w=[ptk])
                    wi = (c * 4 + q) % 2
                    cpy(WS[wi][:, 0:1024], pt[:, :], [ptk, 'ws%d' % wi], ['ws%d' % wi])
                    T.dma('sp', ld, ssmc_d[:, c * 4096 + q * 1024: c * 4096 + (q + 1) * 1024], WS[wi][:, 0:1024],
                          r=['ws%d' % wi], w=['scr_wz%d_%d' % (c, q)])
            TMPF = big(84 * KB, F32, 4096)
            a8b = lambda a: a.unsqueeze(2).to_broadcast([128, 32, 128])
            g128 = lambda a: a.rearrange("p (g m) -> p g m", m=128)
            tt(g128(TB), g128(PI_), a8b(A8i), ALU.mult, PIK + ['pwi', 'tbh0', 'tbh1'], ['tbx'])
            tt(g128(TMPF), g128(PR), a8b(A8r), ALU.mult, PRK + ['pwr'], ['tmpf'])
            tt(TMPF, TMPF, TB, ALU.subtract, ['tmpf', 'tbx'], ['tmpf'])
            cpy(BZr, TMPF, ['tmpf', 'bzr'], ['bzr'])
            tt(g128(TB), g128(PR), a8b(A8i), ALU.mult, PRK + ['pwi', 'tmpf'], ['tbx'])
            tt(g128(TMPF), g128(PI_), a8b(A8r), ALU.mult, PIK + ['pwr', 'bzr'], ['tmpf'])
            tt(TMPF, TMPF, TB, ALU.add, ['tmpf', 'tbx'], ['tmpf'])
            cpy(BZi, TMPF, ['tmpf', 'bzi'], ['bzi'])
            for c, BZ in enumerate((BZr, BZi)):
                for q in range(4):
                    pt, ptk = bankb(4 + (c * 4 + q) % 2)
                    for gi in range(8):
                        g = q * 8 + gi
                        T.op('pe', lambda e, BZ=BZ, g=g, gi=gi, pt=pt: e.transpose(
                            pt[:, gi * 128:(gi + 1) * 128], BZ[:, g * 128:(g + 1) * 128], IDB[:]),
                            r=['bzr', 'bzi', 'idb'], w=[ptk])
                    wi = (c * 4 + q) % 2
                    cpy(WS[wi][:, 0:1024], pt[:, :], [ptk, 'ws%d' % wi], ['ws%d' % wi])
                    T.dma('sp', ld, ssmc_d[:, 20480 + c * 4096 + q * 1024: 20480 + c * 4096 + (q + 1) * 1024], WS[wi][:, 0:1024],
                          r=['ws%d' % wi], w=['scr_wz2%d_%d' % (c, q)])
            cprod(CRE, CIM, (True, 8), (False, 15), rkc)
            cpy(WS[0][:], PR, PRK + ['ws0'], ['ws0']); neg(WS[1][:], PI_, PIK + ['ws1'], ['ws1'])
            cprod(KBR, KBI, (False, 7), (True, 7), rkb)
            cpy(L_r, PR, PRK, ['lr']); cpy(L_i, PI_, PIK, ['li'])
            cprod(CRE, CIM, (True, 7), (False, 7), rkc)
            R_rf, R_nif = R_r, R_ni
            R_rv = TB[:, 0:2048].bitcast(BF16); R_niv = TB[:, 2048:4096].bitcast(BF16)
            H0, H1 = slice(0, 64), slice(64, 128)
            cpy(R_rf[H0], PR[H0], PRK, ['rr0']); neg(R_nif[H0], PI_[H0], PIK, ['rni0'])
            T.op('dve', lambda e: e.memset(R_rf[H1], 0.0), w=['rr1']); T.op('dve', lambda e: e.memset(R_nif[H1], 0.0), w=['rni1'])
            cpy(R_rv[H1], PR[H1], PRK + ['tbh0', 'tbh1'], ['rv1']); neg(R_niv[H1], PI_[H1], PIK + ['tbh0', 'tbh1'], ['rnv1'])
            T.op('dve', lambda e: e.memset(R_rv[H0], 0.0), r=['tbh0', 'tbh1'], w=['rv0']); T.op('dve', lambda e: e.memset(R_niv[H0], 0.0), r=['tbh0', 'tbh1'], w=['rnv0'])
            T.dma('sp', ld, ssmc_d[:, 8192:12288], WS[0][:], r=['ws0'], w=['scr_wy0'])
            T.dma('sp', ld, ssmc_d[:, 12288:16384], WS[1][:], r=['ws1'], w=['scr_wy1'])
            ck(5)
            ck(6)
            MFv = CTb[:, O_MF:O_MF + 128]; MRv = CTb[:, O_MR:O_MR + 128]; IDv = CTb[:, O_ID:O_ID + 128]
            for g in range(32):
                pA, pAk = bank(g % 2)
                A_ = pA[:, 0:128]; B_ = pA[:, 128:256]
                gs = slice(g * 128, (g + 1) * 128)
                for (dst, Rr_, Rn_) in ((A_, R_rf, R_nif), (B_, R_rv, R_niv)):
                    T.op('pe', lambda e, dst=dst, Rr_=Rr_, gs=gs: e.matmul(dst, lhsT=L_r[:, gs], rhs=Rr_[:, gs], start=True, stop=False),
                         r=['lr', 'rr0', 'rr1', 'rv0', 'rv1'], w=[pAk])
                    T.op('pe', lambda e, dst=dst, Rn_=Rn_, gs=gs: e.matmul(dst, lhsT=L_i[:, gs], rhs=Rn_[:, gs], start=False, stop=True),
                         r=['li', 'rni0', 'rni1', 'rnv0', 'rnv1'], w=[pAk])
                t1 = SMT[0][:, 0:128]; t2 = SMT[1][:, 0:128]
                tt(t1, A_, MFv, ALU.mult, [pAk, 'ct', 'smt0'], ['smt0'])
                tt(t2, B_, MRv, ALU.mult, [pAk, 'ct', 'smt1'], ['smt1'])
                tt(t1, t1, t2, ALU.add, ['smt0', 'smt1'], ['smt0'])
                wi = (g // 8) % 2
                T.op('dve', lambda e, g=g, wi=wi, t1=t1: e.scalar_tensor_tensor(
                    out=WS[wi][:, (g % 8) * 128:(g % 8 + 1) * 128], in0=IDv, scalar=SMk[:, O_DSH + g:O_DSH + g + 1], in1=t1,
                    op0=ALU.mult, op1=ALU.add), r=['smt0', 'ct', 'smk', 'ws%d' % wi], w=['ws%d' % wi])
                if g % 8 == 7:
                    q = g // 8
                    T.dma('sp', ld, ssmc_d[:, 16384 + q * 1024:16384 + (q + 1) * 1024], WS[wi][:, 0:1024],
                          r=['ws%d' % wi], w=['scr_t%d' % q])
            ck(7)
            if debug:
                dbg['pwr'] = (PWR, [128, 512], F32); dbg['pwi'] = (PWI, [128, 512], F32)
                dbg['kbr'] = (KBR, [128, 512], F32); dbg['atab'] = (ATAB[:].rearrange("p a b -> p (a b)"), [128, 256], F32)
                dbg['s1'] = (S1[:].rearrange("p a b -> p (a b)"), [128, 40], F32)
                dbg['mods'] = (MODS[:].rearrange("p a b c -> p (a b c)"), [128, 160], F32)
                dd = nc.dram_tensor("dbg_ssmc", [128, 28672], BF16, kind="ExternalOutput").ap()
                T.dma('sp', ld, dd[:, :], ssmc_d[:, :], r=['scr_wy0', 'scr_wy1'] + ['scr_wz%d_%d' % (c, q) for c in range(2) for q in range(4)] + ['scr_t%d' % q for q in range(4)] + ['scr_wz2%d_%d' % (c, q) for c in range(2) for q in range(4)], w=['dbg_ssmc'])
                dbg_ssmc = dd
                for nm, (ap, shp, dt_) in list(dbg.items()):
                    dd = nc.dram_tensor("dbg_" + nm, shp, dt_, kind="ExternalOutput").ap()
                    T.dma('sp', ld, dd[:, :], ap, r=[], w=['dbg_' + nm])
                    dbg[nm] = dd
                dbg['ssmc'] = dbg_ssmc
            T.fence()

            xs = [newslot(), newslot(), newslot()]
            os_ = [newslot(), newslot()]
            gs_slot = newslot()
            ssmc_slot = newslot()
            def mm(out, lhsT, rhs, start, stop, r, w, **kw):
                T.op('pe', lambda e: e.matmul(out, lhsT=lhsT, rhs=rhs, start=start, stop=stop, **kw), r=r, w=w)

            def actf(out, in_, func, r, w, **kw):
                T.op('act', lambda e: e.activation(out=out, in_=in_, func=func, **kw), r=r, w=w)

            def vtt(eng, out, a, b, op, r, w):
                T.op(eng, lambda e: e.tensor_tensor(out=out, in0=a, in1=b, op=op), r=r, w=w)

            def trn(out, in_, ident, r, w):
                T.op('pe', lambda e: e.transpose(out, in_, ident), r=r, w=w)

            def norm_mod_T(src_ap, src_key, i, col0, Ssc, Bsh, jj, hkey):
                ss = STAT[:, 4 * i:4 * i + 1]; sq = STAT[:, 4 * i + 1:4 * i + 2]; rs = STAT[:, 4 * i + 2:4 * i + 3]
                sk = 'st%d' % i
                actf(XN[i][:], src_ap, AF.Square, [src_key], ['xn%d' % i, sk], accum_out=ss)
                actf(sq, ss, AF.Sqrt, [sk], [sk], scale=1.0 / D, bias=EPST[:, 0:1])
                T.op('dve', lambda e: e.reciprocal(out=rs, in_=sq), r=[sk], w=[sk])
                T.op('dve', lambda e: e.tensor_scalar(out=XN[i][:], in0=src_ap, scalar1=rs, scalar2=None, op0=ALU.mult),
                     r=[src_key, sk], w=['xn%d' % i])
                ptA, ptAk = bankb(i)
                ptB, ptBk = bankb(6 + i)
                NA = 3
                for kt in range(8):
                    pt_, ptk_, o_ = (ptA, ptAk, kt) if kt < NA else (ptB, ptBk, kt - NA)
                    trn(pt_[:, o_ * 128:(o_ + 1) * 128], XN[i][:, kt * 128:(kt + 1) * 128], IDB[:], ['xn%d' % i], [ptk_])

                def part_b():
                    for kt in range(NA):
                        actf(HT[:, kt, col0:col0 + 128], ptA[:, kt * 128:(kt + 1) * 128], AF.Identity, [ptAk], [hkey + 'e'],
                             scale=Ssc[:, kt, jj:jj + 1], bias=Bsh[:, kt, jj:jj + 1])
                    for kt in range(NA, 8):
                        T.op('dve', lambda e, kt=kt: e.scalar_tensor_tensor(
                            out=HT[:, kt, col0:col0 + 128], in0=ptB[:, (kt - NA) * 128:(kt - NA + 1) * 128], scalar=Ssc[:, kt, jj:jj + 1],
                            in1=Bsh[:, kt, jj:jj + 1].to_broadcast([128, 128]), op0=ALU.mult, op1=ALU.add),
                            r=[ptBk], w=[hkey + 'o'])
                return part_b

            bigv = lambda off, dt_, n: big(off, dt_, n)
            UTv = bigv(0, BF16, 32 * 288).rearrange("p (g n) -> p g n", n=288)
            XBv = bigv(18 * KB, BF16, 2 * 32 * 256).rearrange("p (c g k) -> p c g k", c=2, g=32)
            RGv = bigv(50 * KB, F32, 34 * 64).rearrange("p (t n) -> p t n", n=64)
            SSMCv = bigv(59 * KB, BF16, 20480)
            WZv = SSMCv[:, 0:8192].rearrange("p (c g m) -> p c g m", c=2, g=32)
            WYv = SSMCv[:, 8192:16384].rearrange("p (c g m) -> p c g m", c=2, g=32)
            TTv = SSMCv[:, 16384:20480].rearrange("p (g m) -> p g m", g=32)
            Uv = [bigv(100 * KB + u * 8 * KB, BF16, 4096).rearrange("p (s n) -> p s n", s=8) for u in range(2)]
            YAT = bigv(0, BF16, 4 * 2048).rearrange("p (c n) -> p c n", c=4)
            YBT = bigv(16 * KB, BF16, 4 * 2048).rearrange("p (c n) -> p c n", c=4)
            MT = bigv(32 * KB, BF16, 8 * 1024).rearrange("p (c n) -> p c n", c=8)
            ZT = bigv(52 * KB, BF16, 4 * 1024).rearrange("p (c n) -> p c n", c=4)
            CX = bigv(60 * KB, BF16, 2052)
            BSB = bigv(60 * KB + 4104, BF16, 2048)
            X1 = bigv(52 * KB, F32, 16 * 1024).rearrange("p (t n) -> p t n", t=16)
            ACTT = bigv(0, BF16, 4 * 2048).rearrange("p (c n) -> p c n", c=4)
            WDv = bigv(16 * KB, BF16, 4 * 1024).rearrange("p (c n) -> p c n", c=4)
            UAV = [[bigv(24 * KB + (b_ * 2 + av) * 4 * KB, BF16, 2048) for av in range(2)] for b_ in range(2)]
            DGF = [bigv(40 * KB + b_ * 4608, BF16, 18 * 128).rearrange("p (t n) -> p t n", t=18) for b_ in range(2)]
            SMTB = [SMT[i_][:, 0:256].bitcast(BF16) for i_ in range(4)]
            wsl = [[newslot() for _ in range(4)] for _ in range(2)]
            wd_slot = newslot()
            wz2_slot = newslot()
            HTK = lambda tb: 'hTb%d' % tb
            HTR = lambda tb: ['hTb%de' % tb, 'hTb%do' % tb]
            HTCR = ['hTce', 'hTco']
            wkeys = lambda wi_: ['ws%d' % wi_, 'ws%da' % wi_, 'ws%db' % wi_, 'ws%dc' % wi_]
            ws_n = [0]

            def next_ws():
                i = ws_n[0] % 2
                ws_n[0] += 1
                return i

            for j in range(NB):
                jc = 4
                pend = None
                for t_ in range(18):
                    i = t_ % 2
                    xi = t_ % 3 if j == 0 else (t_ + 1) % 3
                    src = ctx_d[j, t_ * 128:(t_ + 1) * 128, :] if t_ < 2 else x_d[j, (t_ - 2) * 128:(t_ - 1) * 128, :]
                    if j == 0 or t_ >= 2:
                        T.dma('sp', xs[xi], XT[xi][:], src, w=['xt%d' % xi])
                    if t_ == 2:
                        T.dma('sp', ssmc_slot, SSMCv, ssmc_d[:, 0:20480], w=['ssmc'])
                    hkey = 'hTc' if t_ < 2 else HTK((t_ - 2) // 4)
                    pb_ = norm_mod_T(XT[xi][:], 'xt%d' % xi, i, t_ * 128, S1, MODS[:, 0], jc if t_ < 2 else j, hkey)
                    if pend is not None:
                        pend()
                    pend = pb_
                pend()
                if debug and j == 0:
                    dump('ht', HT[:].rearrange("p k n -> p (k n)"), [128, 8 * NTOK], BF16, HTCR + [k_ for b_ in range(4) for k_ in HTR(b_)])
                ck(10)
                wi = next_ws()
                T.dma('pool', wsl[wi][0], WS[wi][:].rearrange("p (k n) -> p k n", n=512),
                      win_d[:, 0:512].rearrange("(k p) n -> p k n", p=128), w=wkeys(wi))
                WUv = WS[wi][:].rearrange("p (k n) -> p k n", n=512)
                pbk = [0]
                for bi, (M, base, cbase, hk) in enumerate(((32, 0, 0, HTCR), (128, 256, 32, HTR(0) + HTR(1)), (128, 1280, 160, HTR(2) + HTR(3)))):
                    ub = bi % 2
                    for s in range(8):
                        ps, psk = bank(2 + pbk[0] % 4); pbk[0] += 1
                        for kt in range(8):
                            mm(ps[0:M, :], HT[:, kt, base + s: base + 8 * M: 8], WUv[:, kt, :], kt == 0, kt == 7,
                               hk + ['ws%d' % wi], [psk])
                        actf(Uv[ub].rearrange("p s n -> p (s n)").rearrange("p (g s h) -> p g s h", g=32, s=8)[0:M, :, s, :],
                             ps[0:M, :].rearrange("p (g h) -> p g h", h=16), AF.Copy, [psk], ['U%d' % ub])
                    for q in range(4):
                        ptb, ptk = bankb(q % 2)
                        for gi in range(8):
                            g = q * 8 + gi
                            trn(ptb[:, gi * 128: gi * 128 + M], Uv[ub].rearrange("p s n -> p (s n)")[0:M, g * 128:(g + 1) * 128], IDB[0:M, 0:M],
                                ['U%d' % ub], [ptk])
                        T.op('dve', lambda e, ptb=ptb, q=q, M=M, cbase=cbase: e.tensor_copy(
                            out=UTv[:, q * 8:(q + 1) * 8, cbase:cbase + M],
                            in_=ptb[:, :].rearrange("p (g m) -> p g m", m=128)[:, :, 0:M]), r=[ptk], w=['UT'])
                if debug and j == 0:
                    dump('ut', UTv.rearrange("p g n -> p (g n)"), [128, 32 * 288], BF16, ['UT'])
                ck(11)
                WZ2v = bigv(100 * KB, BF16, 8192).rearrange("p (c g m) -> p c g m", c=2, g=32)
                T.dma('sp', wz2_slot, bigv(100 * KB, BF16, 8192), ssmc_d[:, 20480:28672], w=['U0', 'U1', 'wz2'])
                T.op('dve', lambda e: e.memset(RGv[:, 0:2, :], 0.0), w=['rg'])
                ACAT = ATAB[:, 0, :]; ASW = ATAB[:, 1, :]
                T1 = SMT[0][:, 0:128]; T2 = SMT[1][:, 0:128]
                c4 = lambda a: a.rearrange("p (t c g) -> p t c g", t=2, c=2)
                SL = 32
                rsl = lambda h_, l_: slice(h_, (l_ - 1 if l_ > 0 else None), -1)
                for sg in range(NCH // SL):
                    zp = PB[sg % 2][:, :].rearrange("p (c g t) -> p c g t", c=2, g=32)
                    zk = ['ps%d' % (4 * (sg % 2) + u) for u in range(4)]
                    if sg == 0:
                        hi, lo = 31, 0
                    else:
                        hi, lo = 319 - SL * sg, 288 - SL * sg
                    for c in range(2):
                        for g in range(32):
                            t0 = 1 if sg == 0 else 0
                            mm(zp[0:64, c, g, :], WZv[:, c, g, 0:64], UTv[:, g, SL * sg:SL * (sg + 1)], True, False, ['UT', 'ssmc'], zk)
                            mm(zp[0:64, c, g, t0:SL], WZ2v[:, c, g, 0:64], UTv[:, g, SL * sg - 1 + t0:SL * (sg + 1) - 1], False, True, ['UT', 'wz2'], zk)
                            mm(zp[64:128, c, g, :], WZv[:, c, g, 64:128], UTv[:, g, rsl(hi, lo)], True, False,
                               ['UT', 'ssmc'], zk, tile_position=(0, 64))
                            if sg == 0:
                                mm(zp[64:128, c, g, 1:SL], WZ2v[:, c, g, 64:128], UTv[:, g, rsl(hi, lo + 1)], False, True,
                                   ['UT', 'wz2'], zk, tile_position=(0, 64))
                            elif sg == 1:
                                mm(zp[64:128, c, g, 0:1], WZ2v[:, c, g, 64:128], UTv[:, g, 0:1], False, False,
                                   ['UT', 'wz2'], zk, tile_position=(0, 64))
                                mm(zp[64:128, c, g, 1:SL], WZ2v[:, c, g, 64:128], UTv[:, g, rsl(hi, lo + 1)], False, True,
                                   ['UT', 'wz2'], zk, tile_position=(0, 64))
                            else:
                                mm(zp[64:128, c, g, :], WZ2v[:, c, g, 64:128], UTv[:, g, rsl(hi + 1, lo + 1)], False, True,
                                   ['UT', 'wz2'], zk, tile_position=(0, 64))
                    R = RGv
                    if sg > 0:
                        T.op('dve', lambda e: e.tensor_copy(out=RGv[:, 0:2, :], in_=RGv[:, SL:SL + 2, :]), r=['rg'], w=['rg'])
                    for m_ in range(SL // 2):
                        t_ = 2 * m_
                        prev = R[:, t_:t_ + 2, :]
                        cur = R[:, t_ + 2:t_ + 4, :]
                        p4 = prev.rearrange("p t (c g) -> p t c g", c=2)
                        vtt('dve', c4(T1), p4, c4(ACAT), ALU.mult, ['rg'], ['sT1'])
                        vtt('dve', c4(T2), p4[:, :, ::-1, :], c4(ASW), ALU.mult, ['rg'], ['sT2'])
                        vtt('dve', c4(T1), c4(T1), zp[:, :, :, t_:t_ + 2].rearrange("p c g t -> p t c g"), ALU.add, ['sT1'] + zk, ['sT1'])
                        vtt('dve', cur.rearrange("p t (c g) -> p t c g", c=2), c4(T1), c4(T2), ALU.add, ['sT1', 'sT2'], ['rg'])
                    i0 = max(SL * sg, 31); i1 = min(SL * sg + SL - 1, 286)
                    if i0 <= i1:
                        n = i1 - i0 + 1; t0 = i0 - SL * sg
                        srcv = lambda hs: R[hs, 2 + t0:2 + t0 + n, :].rearrange("p t (c g) -> p c g t", c=2)
                        actf(XBv[0:64, :, :, i0 - 31:i0 - 31 + n], srcv(slice(0, 64)), AF.Copy, ['rg'], ['XB'])
                        kh, kl = 286 - i0, 286 - i1
                        actf(XBv[64:128, :, :, kh:(kl - 1 if kl > 0 else None):-1], srcv(slice(64, 128)), AF.Copy, ['rg'], ['XB'])
                if debug and j == 0:
                    dump('xb', XBv.rearrange("p c g k -> p (c g k)"), [128, 2 * 32 * 256], BF16, ['XB'])
                ck(12)
                for blk in range(2):
                    for q in range(8):
                        yq, yk = bank(4 + q % 4)
                        for gi in range(4):
                            g = 4 * q + gi
                            o_ = yq[:, gi * 128:(gi + 1) * 128]
                            mm(o_, UTv[:, g, 32 + blk * 128:32 + (blk + 1) * 128], TTv[:, g, :], True, False, ['UT', 'ssmc'], [yk])
                            mm(o_, XBv[:, 0, g, blk * 128:(blk + 1) * 128], WYv[:, 0, g, :], False, False, ['XB', 'ssmc'], [yk])
                            mm(o_, XBv[:, 1, g, blk * 128:(blk + 1) * 128], WYv[:, 1, g, :], False, True, ['XB', 'ssmc'], [yk])
                        actf(Uv[blk][:, :, q * 64:(q + 1) * 64].rearrange("p s (g h) -> p g s h", g=4),
                             yq[:, :].rearrange("p (g s h) -> p g s h", g=4, s=8), AF.Gelu_apprx_tanh, [yk], ['ZU%d' % blk])
                if debug and j == 0:
                    dump('zu', Uv[0].rearrange("p s n -> p (s n)"), [128, 4096], BF16, ['ZU0'])
                ck(13)
                T.fence()
                wi = next_ws()
                T.dma('pool', wsl[wi][0], WS[wi][:, 0:2048].rearrange("p (k n) -> p k n", n=512),
                      gluw_d[:, :].rearrange("(k p) n -> p k n", p=128), w=wkeys(wi))
                GWv = WS[wi][:, 0:2048].rearrange("p (k n) -> p k n", n=512)
                pbk = [0]
                for blk in range(2):
                    for ct in range(4):
                        ptb, ptk = bankb(ct % 2)
                        for s in range(8):
                            trn(ptb[:, s * 128:(s + 1) * 128], Uv[blk][:, s, ct * 128:(ct + 1) * 128], IDB[:], ['ZU%d' % blk], [ptk])
                        T.op('dve', lambda e, ptb=ptb, ct=ct: e.tensor_copy(
                            out=ZT[:, ct, :].rearrange("p (k s) -> p s k", s=8),
                            in_=ptb[:, :].rearrange("p (s k) -> p s k", s=8)), r=[ptk], w=['zT'])
                    for co in range(4):
                        for tb2 in range(2):
                            ps, psk = bank(2 + pbk[0] % 4); pbk[0] += 1
                            sgi = pbk[0] % 2
                            for ci in range(4):
                                mm(ps, GWv[:, ci, co * 128:(co + 1) * 128], ZT[:, ci, tb2 * 512:(tb2 + 1) * 512], ci == 0, ci == 3,
                                   ['zT', 'ws%d' % wi], [psk])
                            actf(SMTB[sgi], ps, AF.Sigmoid, [psk], ['smt%d' % sgi], bias=SMk[:, O_GLUB + co:O_GLUB + co + 1])
                            vtt('dve', YAT[:, co, blk * 1024 + tb2 * 512: blk * 1024 + (tb2 + 1) * 512],
                                ZT[:, co, tb2 * 512:(tb2 + 1) * 512], SMTB[sgi], ALU.mult, ['zT', 'smt%d' % sgi], ['yaT'])
                if debug and j == 0:
                    dump('yat', YAT.rearrange("p c n -> p (c n)"), [128, 8192], BF16, ['yaT'])
                ck(14)
                T.op('dve', lambda e: e.memset(CX[:, 0:1], 0.0), w=['cx'])
                T.op('dve', lambda e: e.memset(CX[:, 2049:2050], 0.0), w=['cx'])
                w3 = win_d[:, 512:2048].rearrange("(k p) (jj c n) -> p k jj c n", p=128, jj=3, c=4)
                pbk = [0]
                for ct in range(4):
                    wi = next_ws()
                    Wv = WS[wi][:, 0:3072].rearrange("p (k jj n) -> p k jj n", k=8, jj=3)
                    wk3 = ['ws%d' % wi, 'ws%da' % wi, 'ws%db' % wi]
                    for jj in range(3):
                        c0_ = 512 + jj * 512 + ct * 128
                        T.dma('pool', wsl[wi][jj], Wv[:, :, jj, :], win_d[:, c0_:c0_ + 128].rearrange("(k p) n -> p k n", p=128),
                              w=(wkeys(wi) if jj == 0 else [wk3[jj]]))
                    for tb in range(4):
                        cols = slice(256 + tb * 512, 256 + (tb + 1) * 512)
                        bks = [bank(2 + (pbk[0] + u) % 6) for u in range(3)]; pbk[0] += 3
                        for jj in range(3):
                            for kt in range(8):
                                mm(bks[jj][0], Wv[:, kt, jj, :], HT[:, kt, cols], kt == 0, kt == 7, HTR(tb) + [wk3[jj]], [bks[jj][1]])
                        actf(SMT[2][:, :], bks[1][0], AF.Copy, [bks[1][1]], ['smt2'])
                        vtt('dve', CX[:, 1 + tb * 512:1 + (tb + 1) * 512], SMT[2][:, :], bks[2][0], ALU.mult, ['smt2', bks[2][1]], ['cx'])
                        actf(BSB[:, tb * 512:(tb + 1) * 512], bks[0][0], AF.Copy, [bks[0][1]], ['bsb'])
                    for tb in range(4):
                        cps, cpk = bank(2 + pbk[0] % 6); pbk[0] += 1
                        for k in range(3):
                            mm(cps, DGS[:, ct * 3 + k, :], CX[:, tb * 512 + k: tb * 512 + k + 512], k == 0, k == 2, ['cx'], [cpk])
                        vtt('dve', YBT[:, ct, tb * 512:(tb + 1) * 512], BSB[:, tb * 512:(tb + 1) * 512], cps, ALU.mult, ['bsb', cpk], ['ybT'])
                if debug and j == 0:
                    dump('ybt', YBT.rearrange("p c n -> p (c n)"), [128, 8192], BF16, ['ybT'])
                ck(15)
                wg = win_d[:, 2048:4096].rearrange("(k p) (jj t n) -> p k jj t n", p=128, jj=2, t=8)
                for blk in range(2):
                    for jt in range(8):
                        wi = next_ws()
                        Wg = WS[wi][:, 0:2048].rearrange("p (k jj n) -> p k jj n", k=8, jj=2)
                        PAv = WS[wi][:, 2048:2560].rearrange("p (k n) -> p k n", k=4)
                        PBv = WS[wi][:, 2560:3072].rearrange("p (k n) -> p k n", k=4)
                        wk = 'ws%d' % wi
                        T.dma('pool', wsl[wi][0], Wg[:, :, 0, :], win_d[:, 2048 + jt * 128:2048 + (jt + 1) * 128].rearrange("(k p) n -> p k n", p=128), w=wkeys(wi))
                        T.dma('pool', wsl[wi][3], Wg[:, :, 1, :], win_d[:, 3072 + jt * 128:3072 + (jt + 1) * 128].rearrange("(k p) n -> p k n", p=128), w=[wk + 'c'])
                        T.dma('pool', wsl[wi][1], PAv, pa_d[:, jt * 128:(jt + 1) * 128].rearrange("(k p) n -> p k n", p=128), w=[wk + 'a'])
                        T.dma('pool', wsl[wi][2], PBv, pb_d[:, jt * 128:(jt + 1) * 128].rearrange("(k p) n -> p k n", p=128), w=[wk + 'b'])
                        for tb2 in range(2):
                            tok = slice(blk * 1024 + tb2 * 512, blk * 1024 + (tb2 + 1) * 512)
                            tbk = HTR(blk * 2 + tb2)
                            cols = slice(256 + tok.start, 256 + tok.stop)
                            b0 = 4 * ((jt * 2 + tb2) % 2)
                            (GA, gak), (GB, gbk), (AA, aak), (BB, bbk) = [bank(b0 + u) for u in range(4)]
                            for kt in range(8):
                                mm(GA, Wg[:, kt, 0, :], HT[:, kt, cols], kt == 0, kt == 7, tbk + [wk], [gak])
                            for kt in range(8):
                                mm(GB, Wg[:, kt, 1, :], HT[:, kt, cols], kt == 0, kt == 7, tbk + [wk + 'c'], [gbk])
                            for ci in range(4):
                                mm(AA, PAv[:, ci, :], YAT[:, ci, tok], ci == 0, ci == 3, ['yaT', wk + 'a'], [aak])
                            for ci in range(4):
                                mm(BB, PBv[:, ci, :], YBT[:, ci, tok], ci == 0, ci == 3, ['ybT', wk + 'b'], [bbk])
                            actf(SMT[0][:, :], GA, AF.Sigmoid, [gak], ['smt0'])
                            actf(SMT[1][:, :], GB, AF.Sigmoid, [gbk], ['smt1'])
                            vtt('dve', SMT[2][:, :], SMT[0][:, :], AA, ALU.mult, ['smt0', aak], ['smt2'])
                            vtt('dve', SMT[3][:, :], SMT[1][:, :], BB, ALU.mult, ['smt1', bbk], ['smt3'])
                            vtt('dve', MT[:, jt, tb2 * 512:(tb2 + 1) * 512], SMT[2][:, :], SMT[3][:, :], ALU.add, ['smt2', 'smt3'], ['mT'])
                    if debug and j == 0 and blk == 0:
                        dump('mt', MT.rearrange("p c n -> p (c n)"), [128, 8192], BF16, ['mT'])
                    T.dma('sp', gs_slot, GT[:], gsc_d[j * 2 + 0, :, :], w=['gt'])
                    for half in range(2):
                        T.dma('pool', wsl[half][0], WS[half][:].rearrange("p (k n) -> p k n", n=512),
                              wo_d[:, half * 512:(half + 1) * 512].rearrange("(k p) n -> p k n", p=128), w=wkeys(half))
                        T.op('dve', lambda e, half=half: e.tensor_tensor(
                            out=WS[half][:].rearrange("p (k n) -> p k n", n=512), in0=WS[half][:].rearrange("p (k n) -> p k n", n=512),
                            in1=GT[:, half * 512:(half + 1) * 512].unsqueeze(1).to_broadcast([128, 8, 512]), op=ALU.mult),
                            r=['gt'], w=['ws%d' % half])
                    ws_n[0] = 0
                    pbk = [0]
                    pend = None

                    def stage_w(tt_):
                        tile = blk * 8 + tt_
                        xi = tt_ % 3
                        T.dma('sp', xs[xi], XT[xi][:], x_d[j, tile * 128:(tile + 1) * 128, :], w=['xt%d' % xi])
                        for half in range(2):
                            ps, psk = bank(2 + pbk[0] % 4); pbk[0] += 1
                            WOv = WS[half][:].rearrange("p (k n) -> p k n", n=512)
                            for kt in range(8):
                                mm(ps, MT[:, kt, tt_ * 128:(tt_ + 1) * 128], WOv[:, kt, :], kt == 0, kt == 7, ['mT', 'ws%d' % half], [psk])
                            vtt('dve', X1[:, tile, half * 512:(half + 1) * 512], XT[xi][:, half * 512:(half + 1) * 512], ps, ALU.add,
                                ['xt%d' % xi, psk], ['x1_%d' % tile])

                    def stage_n(tt_):
                        tile = blk * 8 + tt_
                        return norm_mod_T(X1[:, tile, :], 'x1_%d' % tile, tt_ % 2, 256 + tile * 128, S2, MODS[:, 2], j, HTK(tile // 4))
                    for tt_ in range(10):
                        if tt_ < 8:
                            stage_w(tt_)
                        if tt_ >= 2:
                            pb_ = stage_n(tt_ - 2)
                            if pend is not None:
                                pend()
                            pend = pb_
                    pend()
                if debug and j == 0:
                    dump('x1', X1.rearrange("p t n -> p (t n)"), [128, 16384], F32, ['x1_%d' % t_ for t_ in range(16)])
                    dump('h2t', HT[:].rearrange("p k n -> p (k n)"), [128, 8 * NTOK], BF16, HTCR + [k_ for b_ in range(4) for k_ in HTR(b_)])
                ck(16)
                T.fence()
                T.dma('sp', gs_slot, GT[:], gsc_d[j * 2 + 1, :, :], w=['gt'])
                wu = wup_d[:, :].rearrange("(k p) (jj m) -> p k jj m", p=128, jj=2)
                gsz = [4, 4, 4, 4, 3, 3]
                taps = [(1, 1)] + [(a_, b_) for a_ in range(3) for b_ in range(3) if (a_, b_) != (1, 1)]
                DVE_TAPS = []
                i_pair = 0
                pb_up = [0]; pb_cv = [0]
                for gi_, n_g in enumerate(gsz):
                    i0 = i_pair
                    for li in range(n_g):
                        ip = i0 + li
                        bf = ip % 2
                        wi = next_ws()
                        wk = 'ws%d' % wi
                        Wup = WS[wi][:, 0:2048].rearrange("p (k jj n) -> p k jj n", k=8, jj=2)
                        for av in range(2):
                            c0_ = av * FFH + ip * 128
                            T.dma('pool', wsl[wi][av], Wup[:, :, av, :], wup_d[:, c0_:c0_ + 128].rearrange("(k p) n -> p k n", p=128),
                                  w=(wkeys(wi) if av == 0 else [wk + 'a']))
                        if li == 0:
                            T.dma('pool', wd_slot, WDv[:, 0:n_g, :], wdn_d[i0 * 128:(i0 + n_g) * 128, :].rearrange("(k p) n -> p k n", p=128), w=['wd'])
                        def build_dgf(ipp, part=None):
                            bfp = ipp % 2
                            lst = [(av_, tp) for av_ in range(2) for tp in range(9) if tp != 4]
                            if part is not None:
                                lst = lst[part::4]
                            for (av_, tp) in lst:
                                tl_ = ipp + 22 * av_
                                T.op('dve', lambda e, bfp=bfp, av_=av_, tp=tp, tl_=tl_: e.tensor_scalar(
                                    out=DGF[bfp][:, av_ * 9 + tp, :], in0=IDB[:], scalar1=SMk[:, O_CONVW + tl_ * 9 + tp:O_CONVW + tl_ * 9 + tp + 1],
                                    scalar2=None, op0=ALU.mult), r=[], w=['dgf%d' % bfp])
                        if ip == 0:
                            build_dgf(0)
                        for av in range(2):
                            for tb in range(4):
                                ps, psk = bank(pb_up[0] % 4); pb_up[0] += 1
                                for kt in range(8):
                                    mm(ps, Wup[:, kt, av, :], HT[:, kt, 256 + tb * 512:256 + (tb + 1) * 512], kt == 0, kt == 7,
                                       HTR(tb) + [wk if av == 0 else wk + 'a'], [psk])
                                actf(UAV[bf][av][:, tb * 512:(tb + 1) * 512], ps, AF.Copy, [psk], ['uav%d%d' % (bf, av)])
                        for tb in range(4):
                            cvb = []
                            for av in range(2):
                                cps, cpk = bank(4 + pb_cv[0] % 4); pb_cv[0] += 1
                                cvb.append((cps, cpk))
                                src3 = UAV[bf][av].rearrange("p (r c) -> p r c", c=64)
                                dst3 = cps.rearrange("p (r c) -> p r c", c=64)
                                tl = ip + 22 * av
                                first = (2, 1) if tb < 3 else (0, 1)
                                pe_taps = [first] + [t_ for t_ in taps if t_ not in ((1, 1), first)]
                                for ti_, (ta, tb_) in enumerate(pe_taps):
                                    dr, dc = ta - 1, tb_ - 1
                                    r0 = max(8 * tb, -dr); r1 = min(8 * tb + 8, 32 - dr)
                                    c0 = max(0, -dc); c1 = min(64, 64 - dc)
                                    if ti_ == 0:
                                        assert (r0, r1, c0, c1) == (8 * tb, 8 * tb + 8, 0, 64)
                                    mm(dst3[:, r0 - 8 * tb:r1 - 8 * tb, c0:c1], DGF[bf][:, av * 9 + ta * 3 + tb_, :],
                                       src3[:, r0 + dr:r1 + dr, c0 + dc:c1 + dc], ti_ == 0, ti_ == len(pe_taps) - 1,
                                       ['uav%d%d' % (bf, av), 'dgf%d' % bf], [cpk])
                            sgi = tb % 2
                            wc = lambda av_: SMk[:, O_CONVW + (ip + 22 * av_) * 9 + 4:O_CONVW + (ip + 22 * av_) * 9 + 5]
                            ta_ = SMT[2 * sgi][:, :]; tv_ = SMT[2 * sgi + 1][:, :]
                            sa_ = XN[sgi][:, 0:512]
                            T.op('dve', lambda e, ta_=ta_, c_=cvb[0][0], u_=UAV[bf][0][:, tb * 512:(tb + 1) * 512], w_=wc(0): e.scalar_tensor_tensor(
                                out=ta_, in0=u_, scalar=w_, in1=c_, op0=ALU.mult, op1=ALU.add),
                                r=['uav%d0' % bf, cvb[0][1]], w=['smt%d' % (2 * sgi)])
                            actf(sa_, ta_, AF.Silu, ['smt%d' % (2 * sgi)], ['xn%d' % sgi])
                            T.op('dve', lambda e, tv_=tv_, c_=cvb[1][0], u_=UAV[bf][1][:, tb * 512:(tb + 1) * 512], w_=wc(1): e.scalar_tensor_tensor(
                                out=tv_, in0=u_, scalar=w_, in1=c_, op0=ALU.mult, op1=ALU.add),
                                r=['uav%d1' % bf, cvb[1][1]], w=['smt%d' % (2 * sgi + 1)])
                            vtt('dve', ACTT[:, li, tb * 512:(tb + 1) * 512], sa_, tv_, ALU.mult, ['xn%d' % sgi, 'smt%d' % (2 * sgi + 1)], ['actT'])
                            if ip + 1 < NPAIR:
                                build_dgf(ip + 1, part=tb)
                        if li == 0:
                            T.op('dve', lambda e, n_g=n_g: e.tensor_tensor(
                                out=WDv[:, 0:n_g, :], in0=WDv[:, 0:n_g, :], in1=GT[:].unsqueeze(1).to_broadcast([128, n_g, 1024]), op=ALU.mult),
                                r=['gt'], w=['wd'])
                    i_pair += n_g
                    last = gi_ == len(gsz) - 1
                    for tt_ in range(16):
                        i = tt_ % 2
                        for half in range(2):
                            ps, psk = bank(pb_up[0] % 4); pb_up[0] += 1
                            for li in range(n_g):
                                mm(ps, ACTT[:, li, tt_ * 128:(tt_ + 1) * 128], WDv[:, li, half * 512:(half + 1) * 512], li == 0, li == n_g - 1,
                                   ['actT', 'wd'], [psk])
                            vtt('dve', X1[:, tt_, half * 512:(half + 1) * 512], X1[:, tt_, half * 512:(half + 1) * 512], ps, ALU.add,
                                [psk], ['x1_%d' % tt_])
                        if last:
                            ss = STAT[:, 4 * i:4 * i + 1]; sq = STAT[:, 4 * i + 1:4 * i + 2]; rs = STAT[:, 4 * i + 2:4 * i + 3]
                            sk = 'st%d' % i
                            actf(XN[i][:], X1[:, tt_, :], AF.Square, ['x1_%d' % tt_], ['xn%d' % i, sk], accum_out=ss)
                            actf(sq, ss, AF.Sqrt, [sk], [sk], scale=1.0 / D, bias=EPST[:, 0:1])
                            T.op('dve', lambda e, rs=rs, sq=sq: e.reciprocal(out=rs, in_=sq), r=[sk], w=[sk])
                            eb, ebk = ((XT[0], 'xt0'), (GT, 'gt'))[i]
                            T.op('dve', lambda e, eb=eb, tt_=tt_, rs=rs: e.scalar_tensor_tensor(
                                out=eb[:], in0=X1[:, tt_, :], scalar=rs, in1=FG[:], op0=ALU.mult, op1=ALU.mult),
                                r=['x1_%d' % tt_, sk], w=[ebk])
                            T.dma('sp', os_[i], out_d[j, tt_ * 128:(tt_ + 1) * 128, :], eb[:], r=[ebk], w=['out%d_%d' % (j, tt_)])
                            if tt_ == 0 and j + 1 < NB:
                                for t2_ in range(2):
                                    T.dma('sp', xs[1 + t2_], XT[1 + t2_][:], ctx_d[j + 1, t2_ * 128:(t2_ + 1) * 128, :], w=['xt%d' % (1 + t2_)])
                ck(17)
                T.fence()

        except StopBuild:
            pass
        T.fence()
        with nc.Block() as block:
            @block.tensor
            def _(e):
                T.emit('pe', e, sems)

            @block.scalar
            def _(e):
                T.emit('act', e, sems)

            @block.vector
            def _(e):
                T.emit('dve', e, sems)

            @block.gpsimd
            def _(e):
                T.emit('pool', e, sems)

            @block.sync
            def _(e):
                T.emit('sp', e, sems)
    return nc, dbg


def prep_core(inp, ci, NB):
    f = np.float32
    b0 = ci * NB
    d = {}
    d["x"] = np.ascontiguousarray(inp["x"][b0:b0 + NB])
    d["ctx"] = np.ascontiguousarray(inp["ctx"][b0:b0 + NB])
    cc = np.concatenate([inp["c"][b0:b0 + NB], np.broadcast_to(inp["c_ctx"][None, :], (5 - NB, D))], 0)
    d["cT"] = np.ascontiguousarray(cc.reshape(5, 8, 128).transpose(2, 1, 0)).astype(f)
    d["mod_w"] = np.ascontiguousarray(inp["mod_w"][0])
    d["modb_row"] = np.ascontiguousarray(inp["mod_b"][0][None, :])
    d["fg_row"] = np.ascontiguousarray(inp["final_g"][None, :])
    sm = np.zeros((128, NSM), f)
    fm = lambda v, nt: np.ascontiguousarray(v.reshape(nt, 128).T)
    sm[:, O_N1G:O_N1G + 8] = fm(inp["norm1_g"][0], 8)
    sm[:, O_N2G:O_N2G + 8] = fm(inp["norm2_g"][0], 8)
    sm[:, O_GLUB:O_GLUB + 4] = fm(inp["ssm_glu_b"][0], 4)
    sm[:, O_SCONV:O_SCONV + 12] = inp["sconv_w"][0].reshape(3, 4, 128).transpose(2, 1, 0).reshape(128, 12)
    sm[:, O_CONVW:O_CONVW + 396] = inp["ffn_conv_w"][0].reshape(9, 44, 128).transpose(2, 1, 0).reshape(128, 396)
    sm[:, O_MODB:O_MODB + 48] = fm(inp["mod_b"][0], 48)
    sm[:, O_DSH:O_DSH + 32] = np.tile(inp["ssm_d"][0].reshape(32, 16).T, (8, 1))
    sm[:, O_LRE:O_LRE + 32] = inp["ssm_lambda_re"][0].transpose(0, 2, 1).reshape(128, 32)
    sm[:, O_LIM:O_LIM + 32] = inp["ssm_lambda_im"][0].transpose(0, 2, 1).reshape(128, 32)
    sm[:, O_LDT:O_LDT + 32] = np.repeat(inp["ssm_log_dt"][0][:, None, :], 64, 1).reshape(128, 32)
    d["smalls"] = sm
    ct = np.zeros((128, NCT), f)
    ct[:, O_ID:O_ID + 128] = np.eye(128, dtype=f)
    s_idx = np.arange(128) // 16
    ct[:, O_MF:O_MF + 128] = (s_idx[None, :] >= s_idx[:, None]).astype(f)
    ct[:, O_MR:O_MR + 128] = (s_idx[None, :] <= s_idx[:, None]).astype(f)
    ct[:, O_NV:O_NV + 512] = np.repeat(np.arange(-7, 9, dtype=f), 32)[None, :]
    d["consts"] = ct
    bc = np.zeros((128, 4, 32, 16), f)
    bc[:, 0] = inp["ssm_b_re"][0].transpose(0, 2, 1, 3).reshape(128, 32, 16)
    bc[:, 1] = inp["ssm_b_im"][0].transpose(0, 2, 1, 3).reshape(128, 32, 16)
    bc[:, 2] = inp["ssm_c_re"][0].transpose(0, 3, 1, 2).reshape(128, 32, 16)
    bc[:, 3] = inp["ssm_c_im"][0].transpose(0, 3, 1, 2).reshape(128, 32, 16)
    d["ssmbc"] = bc
    d["w_in"] = np.ascontiguousarray(inp["w_in"][0])
    d["glu_w"] = np.ascontiguousarray(inp["ssm_glu_w"][0])
    d["proj_a"] = np.ascontiguousarray(inp["proj_a"][0])
    d["proj_b"] = np.ascontiguousarray(inp["proj_b"][0])
    d["w_out"] = np.ascontiguousarray(inp["w_out"][0])
    d["w_up"] = np.ascontiguousarray(inp["ffn_w_up"][0])
    d["w_down"] = np.ascontiguousarray(inp["ffn_w_down"][0])
    return d


_CACHE = {}


def kernel(**inputs):
    inp = {k: np.asarray(v, dtype=np.float32) for k, v in inputs.items()}
    NB, NCORE = 4, 8
    if "prog" not in _CACHE:
        _CACHE["prog"] = build_program(NB)[0]
    nc = _CACHE["prog"]
    in_maps = [prep_core(inp, ci, NB) for ci in range(NCORE)]
    res = run_bass_kernel_spmd(nc, in_maps, core_ids=list(range(NCORE)))
    return np.concatenate([r["out"] for r in res.results], axis=0).astype(np.float32)
```

```python
import os
import math
import numpy as np
import concourse.bass as bass
import concourse.mybir as mybir
from concourse.bass_utils import run_bass_kernel_spmd

F32 = mybir.dt.float32
BF16 = mybir.dt.bfloat16
AF = mybir.ActivationFunctionType
ALU = mybir.AluOpType

D = 1024
SEQ = 2048
CTXL = 256
NTOK = SEQ + CTXL
NCH = NTOK // 8
FFH = 2816
NPAIR = 22
EPS = 1e-6
ENG = ['pe', 'act', 'dve', 'pool', 'sp']

O_N1G, O_N2G, O_GLUB, O_SCONV, O_CONVW, O_MODB, O_DSH, O_LRE, O_LIM, O_LDT = 0, 8, 16, 20, 32, 428, 476, 508, 540, 572
NSM = 604
O_ID, O_MF, O_MR, O_NV = 0, 128, 256, 384
NCT = 896


class Slot:
    def __init__(self, sem):
        self.sem = sem
        self.count = 0


class Tracker:
    def __init__(self):
        self.ops = {e: [] for e in ENG}
        self.lastw = {}
        self.readers = {}
        self.observed = {e: set() for e in ENG}
        self.last_compute = {e: None for e in ENG}
        self.outstanding = []

    def _deps(self, eng, r, w):
        toks = []
        for k in r:
            t = self.lastw.get(k)
            if t is not None:
                toks.append(t)
        for k in w:
            t = self.lastw.get(k)
            if t is not None:
                toks.append(t)
            toks.extend(self.readers.get(k, ()))
        best = {}
        for t in toks:
            if t[0] == 'E':
                if t[1] == eng and eng == 'pe':
                    continue
                key = ('E', t[1])
            else:
                key = ('D', id(t[1]))
            if key not in best or best[key][2] < t[2]:
                best[key] = t
        out = list(best.values())
        for t in out:
            if t[0] == 'E':
                self.observed[t[1]].add(t[2])
        return out

    def _commit(self, tok, r, w):
        for k in w:
            self.lastw[k] = tok
            self.readers[k] = []
        for k in r:
            if k not in w:
                self.readers.setdefault(k, []).append(tok)

    def op(self, eng, fn, r=(), w=()):
        waits = self._deps(eng, r, w)
        idx = len(self.ops[eng])
        self.ops[eng].append(dict(kind='op', fn=fn, waits=waits))
        self.last_compute[eng] = idx
        self._commit(('E', eng, idx), r, w)

    def dma(self, eng, slot, out, in_, r=(), w=()):
        waits = self._deps(eng, r, w)
        if slot.count > 0 and not any(t[0] == 'D' and t[1] is slot for t in waits):
            waits.append(('D', slot, slot.count))
        slot.count += 16
        self.ops[eng].append(dict(kind='dma', out=out, in_=in_, slot=slot, waits=waits))
        tok = ('D', slot, slot.count)
        self.outstanding.append(tok)
        self._commit(tok, r, w)

    def fence(self):
        toks = []
        for e in ENG:
            if self.last_compute[e] is not None:
                toks.append(('E', e, self.last_compute[e]))
        best = {}
        for t in self.outstanding:
            k = id(t[1])
            if k not in best or best[k][2] < t[2]:
                best[k] = t
        toks.extend(best.values())
        for e in ENG:
            waits = [t for t in toks if not (t[0] == 'E' and t[1] == e and e in ('pe', 'sp'))]
            for t in waits:
                if t[0] == 'E':
                    self.observed[t[1]].add(t[2])
            self.ops[e].append(dict(kind='nop', waits=waits))
        self.lastw = {}
        self.readers = {}
        self.outstanding = []

    def emit(self, eng, e, sems):
        ranks = {}
        for en in ENG:
            ranks[en] = {idx: i + 1 for i, idx in enumerate(sorted(self.observed[en]))}
        have = {}
        myrank = ranks[eng]
        for idx, o in enumerate(self.ops[eng]):
            for t in o['waits']:
                if t[0] == 'E':
                    sem = sems[t[1]]
                    val = ranks[t[1]][t[2]]
                    hk = t[1]
                else:
                    sem = t[1].sem
                    val = t[2]
                    hk = id(t[1])
                if have.get(hk, 0) >= val:
                    continue
                e.wait_ge(sem, val)
                have[hk] = val
            if o['kind'] == 'op':
                ins = o['fn'](e)
                if idx in myrank:
                    ins.then_inc(sems[eng], 1)
            elif o['kind'] == 'dma':
                e.dma_start(out=o['out'], in_=o['in_']).then_inc(o['slot'].sem, 16)


def _rs(ap, pat, **kw):
    return ap.rearrange(pat, **kw)


class StopBuild(Exception):
    pass


def build_program(NB, debug=False, stop=None):
    def ck(i):
        if stop is not None and i == stop:
            raise StopBuild()
    nc = bass.Bass("TRN2", target_bir_lowering=False)
    dt_in = lambda name, shape: nc.dram_tensor(name, shape, F32, kind="ExternalInput").ap()
    x_d = dt_in("x", [NB, SEQ, D])
    ctx_d = dt_in("ctx", [NB, CTXL, D])
    cT_d = dt_in("cT", [128, 8, 5])
    modw_d = dt_in("mod_w", [D, 6 * D])
    modbrow_d = dt_in("modb_row", [1, 6 * D])
    fgrow_d = dt_in("fg_row", [1, D])
    smalls_d = dt_in("smalls", [128, NSM])
    consts_d = dt_in("consts", [128, NCT])
    ssmbc_d = dt_in("ssmbc", [128, 4, 32, 16])
    win_d = dt_in("w_in", [D, 4096])
    gluw_d = dt_in("glu_w", [512, 512])
    pa_d = dt_in("proj_a", [512, D])
    pb_d = dt_in("proj_b", [512, D])
    wo_d = dt_in("w_out", [D, D])
    wup_d = dt_in("w_up", [D, 2 * FFH])
    wdn_d = dt_in("w_down", [FFH, D])
    out_d = nc.dram_tensor("out", [NB, SEQ, D], F32, kind="ExternalOutput").ap()
    ssmc_d = nc.dram_tensor("ssmc_scr", [128, 28672], BF16).ap()
    gsc_d = nc.dram_tensor("gsc_scr", [NB * 2, 128, D], F32).ap()
    dbg = {}

    T = Tracker()
    import contextlib
    es = contextlib.ExitStack()
    with es:
        sb = lambda name, shape, dt: es.enter_context(nc.sbuf_tensor(name, shape, dt))
        sems = {e: es.enter_context(nc.semaphore("s_" + e)) for e in ENG}
        nslot = [0]

        def newslot():
            nslot[0] += 1
            return Slot(es.enter_context(nc.semaphore("d%d" % nslot[0])))

        SMk = sb("smk", [128, NSM], F32)
        IDB = sb("idb", [128, 128], BF16)
        MODS = sb("mods", [128, 4, 8, 5], F32)
        S1 = sb("s1", [128, 8, 5], F32)
        S2 = sb("s2", [128, 8, 5], F32)
        SCB = sb("scb", [128, 8, 5], BF16)
        FG = sb("fg", [128, D], F32)
        ATAB = sb("atab", [128, 2, 128], F32)
        HT = sb("ht", [128, 8, NTOK], BF16)
        BIGB = 116 * 1024
        BIG = sb("big", [128, BIGB // 2], BF16)
        WS = [sb("ws%d" % i, [128, 4096], BF16) for i in range(2)]
        SMT = [sb("smt%d" % i, [128, 512], F32) for i in range(4)]
        XT = [sb("xt%d" % i, [128, D], F32) for i in range(3)]
        XN = [sb("xn%d" % i, [128, D], BF16) for i in range(2)]
        GT = sb("gt", [128, D], F32)
        STAT = sb("stat", [128, 8], F32)
        DGS = sb("dgs", [128, 12, 128], BF16)
        EPST = sb("epst", [128, 1], F32)
        PB = [es.enter_context(nc.psum_tensor("pb%d" % i, [128, 2048], F32)) for i in range(2)]

        def bank(i):
            return PB[i // 4][:, (i % 4) * 512:(i % 4) * 512 + 512], "ps%d" % i

        def bankb(i):
            return PB[i // 4][:, (i % 4) * 512:(i % 4) * 512 + 512].bitcast(BF16), "ps%d" % i

        def big(off, dtype, n):
            esz = 4 if dtype == F32 else 2
            assert off % 4 == 0 and off + n * esz <= BIGB, (off, n)
            a = BIG[:, off // 2: off // 2 + n * esz // 2]
            if dtype == F32:
                a = a.bitcast(F32)
            return a

        KB = 1024
        ws_slots = [newslot() for _ in range(2)]
        ws_ctr = [0]

        def wload(src_ap, ncols_view, key_extra=None):
            i = ws_ctr[0] % 2
            ws_ctr[0] += 1
            return i

        def cast_dma(slot, dst, src, wkeys):
            T.dma('pool', slot, dst, src, r=(), w=wkeys)

        def dump(nm, ap, shp, dt_, rk):
            dd = nc.dram_tensor("dbg_" + nm, shp, dt_, kind="ExternalOutput").ap()
            T.dma('sp', newslot(), dd[:, :], ap, r=rk, w=['dbg_' + nm])
            dbg[nm] = dd

        try:
          if True:
            pass

            ld = newslot()
            CTb = big(0, F32, NCT)
            SBC = big(4 * KB, F32, 2048)
            T.dma('sp', ld, SMk[:], smalls_d[:, :], w=['smk'])
            T.dma('sp', newslot(), CTb, consts_d[:, :], w=['ct'])
            T.dma('sp', newslot(), SBC, ssmbc_d.rearrange("p a g h -> p (a g h)"), w=['sbc'])
            CTt = XT[0][:, 0:40]
            T.dma('sp', newslot(), XT[0][:, 0:40], cT_d.rearrange("p k j -> p (k j)"), w=['xt0'])
            T.dma('sp', newslot(), FG[:], fgrow_d[0:1, :].partition_broadcast(128), w=['fg'])
            T.op('dve', lambda e: e.tensor_copy(out=IDB[:], in_=CTb[:, O_ID:O_ID + 128]), r=['ct'], w=['idb'])
            T.op('act', lambda e: e.activation(out=SCB[:].rearrange("p k j -> p (k j)"), in_=XT[0][:, 0:40], func=AF.Silu),
                 r=['xt0'], w=['scb'])

            ck(1)
            SCREP = big(16 * KB, BF16, NB * 8 * 128)
            for j in range(NB):
                T.op('dve', lambda e, j=j: e.tensor_copy(
                    out=SCREP[:, j * 1024:(j + 1) * 1024].rearrange("p (k m) -> p k m", m=128),
                    in_=SCB[:, :, j:j + 1].to_broadcast([128, 8, 128])), r=['scb'], w=['screp%d' % j])
            MBR = big(32 * KB, F32, 1024)
            fmk = {0: 0, 1: 1, 3: 2, 4: 3}
            pm, pmk = bank(0)
            for kind in range(6):
                for half in range(2):
                    wi = (kind * 2 + half) % 2
                    c0 = kind * D + half * 512
                    cast_dma(ws_slots[wi], WS[wi][:].rearrange("p (k n) -> p k n", n=512),
                             modw_d[:, c0:c0 + 512].rearrange("(k p) n -> p k n", p=128), ['ws%d' % wi])
                    Wv = WS[wi][:].rearrange("p (k n) -> p k n", n=512)
                    if kind in fmk:
                        for tl in range(4):
                            t8 = half * 4 + tl
                            col = (fmk[kind] * 8 + t8) * 5
                            for k in range(8):
                                T.op('pe', lambda e, Wv=Wv, k=k, tl=tl, col=col: e.matmul(
                                    pm[:, col:col + 5], lhsT=Wv[:, k, tl * 128:(tl + 1) * 128], rhs=SCB[:, k, :],
                                    start=(k == 0), stop=(k == 7)), r=['ws%d' % wi, 'scb'], w=[pmk])
                    else:
                        gi = 0 if kind == 2 else 1
                        if half == 0:
                            T.dma('sp', ld, MBR, modbrow_d[0:1, kind * D:(kind + 1) * D].partition_broadcast(128), w=['mbr'])
                        for j in range(NB):
                            pg, pgk = bank(2 + (j % 2))
                            for k in range(8):
                                T.op('pe', lambda e, Wv=Wv, k=k, j=j, pg=pg: e.matmul(
                                    pg[:, :], lhsT=SCREP[:, j * 1024 + k * 128: j * 1024 + (k + 1) * 128], rhs=Wv[:, k, :],
                                    start=(k == 0), stop=(k == 7)), r=['ws%d' % wi, 'screp%d' % j], w=[pgk])
                            T.op('dve', lambda e, pg=pg, half=half: e.tensor_tensor(
                                out=GT[:, half * 512:(half + 1) * 512], in0=pg[:, :], in1=MBR[:, half * 512:(half + 1) * 512],
                                op=ALU.add), r=[pgk, 'mbr'], w=['gt%d' % half])
                            T.dma('sp', ld, gsc_d[j * 2 + gi, :, half * 512:(half + 1) * 512], GT[:, half * 512:(half + 1) * 512],
                                  r=['gt%d' % half], w=['gsc%d_%d_%d' % (j, gi, half)])
            ck(2)
            for kind, f in fmk.items():
                T.op('dve', lambda e, kind=kind, f=f: e.tensor_tensor(
                    out=MODS[:, f, :, :], in0=pm[:, f * 40:(f + 1) * 40].rearrange("p (k j) -> p k j", j=5),
                    in1=SMk[:, O_MODB + kind * 8:O_MODB + kind * 8 + 8].unsqueeze(2).to_broadcast([128, 8, 5]),
                    op=ALU.add), r=[pmk, 'smk'], w=['mods'])
            for (S, f, og) in ((S1, 1, O_N1G), (S2, 3, O_N2G)):
                T.op('dve', lambda e, S=S, f=f, og=og: e.scalar_tensor_tensor(
                    out=S[:], in0=MODS[:, f, :, :], scalar=1.0,
                    in1=SMk[:, og:og + 8].unsqueeze(2).to_broadcast([128, 8, 5]),
                    op0=ALU.add, op1=ALU.mult), r=['mods', 'smk'], w=['s12'])

            T.op('dve', lambda e: e.memset(EPST[:], EPS), w=['epst'])
            for q_ in range(12):
                T.op('dve', lambda e, q_=q_: e.tensor_scalar(out=DGS[:, q_, :], in0=IDB[:], scalar1=SMk[:, O_SCONV + q_:O_SCONV + q_ + 1],
                                                          scalar2=None, op0=ALU.mult), r=['idb', 'smk'], w=['dgs'])
            ck(3)
            o = 36 * KB
            def alloc(n, dtype=F32):
                nonlocal o
                v = big(o, dtype, n)
                o += n * (4 if dtype == F32 else 2)
                return v
            HTF = HT[:].rearrange("p k n -> p (k n)")
            oh = 0
            def alloch(n, dtype=F32):
                nonlocal oh
                esz = 4 if dtype == F32 else 2
                a = HTF[:, oh // 2: oh // 2 + n * esz // 2]
                if dtype == F32:
                    a = a.bitcast(F32)
                oh += n * esz
                assert oh <= 36 * KB
                return a
            DT = alloch(32); LRD = alloch(32); TH = alloch(32)
            ANG = alloch(512); MAG = alloch(512); TMPA = alloch(512)
            SN = alloch(512); CS = alloch(512)
            LMAG = MAG; PWR = CS; PWI = SN
            AR1 = alloch(32); NRE = alloch(32); NIM = alloch(32); DEN = alloch(32); TQ = alloch(32); KR = alloch(32); KI = alloch(32)
            KBR = alloch(512); KBI = alloch(512); TK = alloch(512)
            NEGPI = alloch(2)
            lre = SMk[:, O_LRE:O_LRE + 32]; lim = SMk[:, O_LIM:O_LIM + 32]; ldt = SMk[:, O_LDT:O_LDT + 32]
            nv = CTb[:, O_NV:O_NV + 512]
            V = 'dve'
            def tt(out, a, b, op, r, w):
                T.op(V, lambda e: e.tensor_tensor(out=out, in0=a, in1=b, op=op), r=r, w=w)
            T.op('dve', lambda e: e.memset(NEGPI, -math.pi), w=['negpi'])
            T.op('act', lambda e: e.activation(out=DT, in_=ldt, func=AF.Exp), r=['smk'], w=['dt'])
            tt(LRD, lre, DT, ALU.mult, ['smk', 'dt'], ['lrd'])
            tt(TH, lim, DT, ALU.mult, ['smk', 'dt'], ['th'])
            b3 = lambda a: a.unsqueeze(1).to_broadcast([128, 16, 32])
            r3 = lambda a: a.rearrange("p (n g) -> p n g", g=32)
            tt(r3(ANG), r3(nv), b3(TH), ALU.mult, ['ct', 'th'], ['ang'])
            tt(r3(LMAG), r3(nv), b3(LRD), ALU.mult, ['ct', 'lrd'], ['lmag'])
            T.op('act', lambda e: e.activation(out=MAG, in_=LMAG, func=AF.Exp), r=['lmag'], w=['mag', 'lmag'])
            TWO_PI = 2.0 * math.pi
            I32 = mybir.dt.int32
            XP = XT[1][:, 0:512]; TF = XT[1][:, 512:1024]; TI = SMT[2][:, :].bitcast(I32)

            def sincos(dst, shift, wk):
                T.op('dve', lambda e: e.tensor_scalar(out=XP, in0=ANG, scalar1=shift + 64.0 * TWO_PI, scalar2=None, op0=ALU.add),
                     r=['ang'], w=['xp'])
                T.op('dve', lambda e: e.tensor_scalar(out=TF, in0=XP, scalar1=1.0 / TWO_PI, scalar2=None, op0=ALU.mult), r=['xp'], w=['tf'])
                T.op('dve', lambda e: e.tensor_copy(out=TI, in_=TF), r=['tf'], w=['ti'])
                T.op('dve', lambda e: e.tensor_copy(out=TF, in_=TI), r=['ti'], w=['tf'])
                T.op('dve', lambda e: e.scalar_tensor_tensor(out=TMPA, in0=TF, scalar=-TWO_PI, in1=XP, op0=ALU.mult, op1=ALU.add),
                     r=['tf', 'xp'], w=['tmpa'])
                T.op('dve', lambda e: e.tensor_scalar(out=TF, in0=TMPA, scalar1=math.pi, scalar2=None, op0=ALU.is_ge), r=['tmpa'], w=['tf'])
                T.op('dve', lambda e: e.scalar_tensor_tensor(out=TMPA, in0=TF, scalar=-TWO_PI, in1=TMPA, op0=ALU.mult, op1=ALU.add),
                     r=['tf', 'tmpa'], w=['tmpa'])
                T.op('dve', lambda e: e.tensor_scalar(out=TMPA, in0=TMPA, scalar1=-math.pi, scalar2=math.pi, op0=ALU.max, op1=ALU.min),
                     r=['tmpa'], w=['tmpa'])
                T.op('act', lambda e: e.activation(out=dst, in_=TMPA, func=AF.Sin), r=['tmpa'], w=[wk])
            sincos(SN, 0.0, 'sn')
            sincos(CS, math.pi / 2, 'cs')
            tt(PWR, MAG, CS, ALU.mult, ['mag', 'cs'], ['pwr', 'cs'])
            tt(PWI, MAG, SN, ALU.mult, ['mag', 'sn'], ['pwi', 'sn'])
            pw = lambda P, n: P[:, n * 32:(n + 1) * 32]
            a_r = pw(PWR, 8); a_i = pw(PWI, 8)
            T.op('dve', lambda e: e.tensor_scalar(out=AR1, in0=a_r, scalar1=-1.0, scalar2=None, op0=ALU.add), r=['pwr'], w=['ar1'])
            tt(NRE, AR1, lre, ALU.mult, ['ar1', 'smk'], ['nre'])
            tt(TQ, a_i, lim, ALU.mult, ['pwi', 'smk'], ['tq'])
            tt(NRE, NRE, TQ, ALU.add, ['nre', 'tq'], ['nre'])
            tt(NIM, a_i, lre, ALU.mult, ['pwi', 'smk'], ['nim'])
            tt(TQ, AR1, lim, ALU.mult, ['ar1', 'smk', 'nre'], ['tq'])
            tt(NIM, NIM, TQ, ALU.subtract, ['nim', 'tq'], ['nim'])
            tt(DEN, lre, lre, ALU.mult, ['smk'], ['den'])
            tt(TQ, lim, lim, ALU.mult, ['smk', 'nim'], ['tq'])
            tt(DEN, DEN, TQ, ALU.add, ['den', 'tq'], ['den'])
            T.op('dve', lambda e: e.reciprocal(out=DEN, in_=DEN), r=['den'], w=['den'])
            tt(KR, NRE, DEN, ALU.mult, ['nre', 'den'], ['kr'])
            tt(KI, NIM, DEN, ALU.mult, ['nim', 'den'], ['ki'])
            g3 = lambda a: a.rearrange("p (g h) -> p g h", h=16)
            bh = lambda a: a.unsqueeze(2).to_broadcast([128, 32, 16])
            BRE = SBC[:, 0:512]; BIM = SBC[:, 512:1024]; CRE = SBC[:, 1024:1536]; CIM = SBC[:, 1536:2048]
            tt(g3(KBR), g3(BRE), bh(KR), ALU.mult, ['sbc', 'kr'], ['kbr'])
            tt(g3(TK), g3(BIM), bh(KI), ALU.mult, ['sbc', 'ki'], ['tk'])
            tt(KBR, KBR, TK, ALU.subtract, ['kbr', 'tk'], ['kbr'])
            tt(g3(KBI), g3(BIM), bh(KR), ALU.mult, ['sbc', 'kr'], ['kbi'])
            tt(g3(TK), g3(BRE), bh(KI), ALU.mult, ['sbc', 'ki', 'kbr'], ['tk'])
            tt(KBI, KBI, TK, ALU.add, ['kbi', 'tk'], ['kbi'])
            A8r = pw(PWR, 15); A8i = pw(PWI, 15)
            A16r = alloch(32); A16i = alloch(32); TQ2 = alloch(32)
            tt(A16r, A8r, A8r, ALU.mult, ['pwr'], ['a16r'])
            tt(TQ2, A8i, A8i, ALU.mult, ['pwi'], ['tq2'])
            tt(A16r, A16r, TQ2, ALU.subtract, ['a16r', 'tq2'], ['a16r'])
            tt(A16i, A8r, A8i, ALU.mult, ['pwr', 'pwi'], ['a16i'])
            T.op('dve', lambda e: e.tensor_scalar(out=A16i, in0=A16i, scalar1=2.0, scalar2=None, op0=ALU.mult), r=['a16i'], w=['a16i'])
            for q_ in range(4):
                T.op('dve', lambda e, q_=q_: e.tensor_copy(out=ATAB[:, 0, q_ * 32:(q_ + 1) * 32], in_=A16r), r=['a16r'], w=['atab'])
                if q_ % 2 == 0:
                    T.op('dve', lambda e, q_=q_: e.tensor_scalar(out=ATAB[:, 1, q_ * 32:(q_ + 1) * 32], in0=A16i, scalar1=-1.0, scalar2=None, op0=ALU.mult),
                         r=['a16i'], w=['atab'])
                else:
                    T.op('dve', lambda e, q_=q_: e.tensor_copy(out=ATAB[:, 1, q_ * 32:(q_ + 1) * 32], in_=A16i), r=['a16i'], w=['atab'])
            ck(4)
            PR = alloc(4096); PI_ = alloc(4096); TB = alloc(4096)
            L_r = alloc(4096, BF16); L_i = alloc(4096, BF16); R_r = alloc(4096, BF16); R_ni = alloc(4096, BF16)
            BZr = alloch(4096, BF16); BZi = alloch(4096, BF16)
            SSMC = big(0, BF16, 1)
            OFF_UT, OFF_XB, OFF_RG, OFF_SSMC, OFF_ZU = 0, 18 * KB, 50 * KB, 59 * KB, 100 * KB
            g4 = lambda a: a.rearrange("p (g s h) -> p g s h", s=8, h=16)

            SELR = SMT[3][:, 0:256]; SELI = SMT[3][:, 256:512]

            def cprod(Xr, Xi, selF, selR, rk):
                for half, (asc, n0) in enumerate((selF, selR)):
                    hs = slice(half * 64, (half + 1) * 64)
                    for (P, SEL) in ((PWR, SELR), (PWI, SELI)):
                        v = r3(P)[hs]
                        v = v[:, n0:n0 + 8, :] if asc else v[:, n0:(n0 - 8 if n0 - 8 >= 0 else None):-1, :]
                        T.op('act', lambda e, v=v, SEL=SEL, hs=hs: e.activation(out=SEL[hs].rearrange("p (s g) -> p s g", g=32), in_=v, func=AF.Copy),
                             r=['pwr', 'pwi', 'sel'], w=['sel'])
                selv = lambda SEL: SEL.rearrange("p (s g) -> p g s", g=32).unsqueeze(3).to_broadcast([128, 32, 8, 16])
                xr = g3(Xr).unsqueeze(2).to_broadcast([128, 32, 8, 16])
                xi = g3(Xi).unsqueeze(2).to_broadcast([128, 32, 8, 16])
                pr = selv(SELR); pi = selv(SELI)
                tt(g4(PR), xr, pr, ALU.mult, rk + ['sel'], ['prh0', 'prh1'])
                tt(g4(TB), xi, pi, ALU.mult, rk + ['sel'], ['tbh0', 'tbh1'])
                tt(PR, PR, TB, ALU.subtract, ['prh0', 'prh1', 'tbh0', 'tbh1'], ['prh0', 'prh1'])
                tt(g4(PI_), xr, pi, ALU.mult, rk + ['sel'], ['pih0', 'pih1'])
                tt(g4(TB), xi, pr, ALU.mult, rk + ['sel', 'prh0'], ['tbh0', 'tbh1'])
                tt(PI_, PI_, TB, ALU.add, ['pih0', 'pih1', 'tbh0', 'tbh1'], ['pih0', 'pih1'])

            PRK = ['prh0', 'prh1']; PIK = ['pih0', 'pih1']
            rkb = ['kbr', 'kbi', 'pwr', 'pwi']
            rkc = ['sbc', 'pwr', 'pwi']
            cpy = lambda out, in_, r, w: T.op('act', lambda e: e.activation(out=out, in_=in_, func=AF.Copy), r=r, w=w)
            neg = lambda out, in_, r, w: T.op('act', lambda e: e.activation(out=out, in_=in_, func=AF.Copy, scale=-1.0), r=r, w=w)
            cprod(KBR, KBI, (False, 14), (True, 7), rkb)
            cpy(BZr, PR, PRK, ['bzr']); cpy(BZi, PI_, PIK, ['bzi'])
            for c, BZ in enumerate((BZr, BZi)):
                for q in range(4):
                    pt, ptk = bankb(4 + (c * 4 + q) % 2)
                    for gi in range(8):
                        g = q * 8 + gi
                        T.op('pe', lambda e, BZ=BZ, g=g, gi=gi, pt=pt: e.transpose(
                            pt[:, gi * 128:(gi + 1) * 128], BZ[:, g * 128:(g + 1) * 128], IDB[:]),
                            r=['bzr', 'bzi', 'idb'], w=[ptk])
                    wi = (c * 4 + q) % 2
                    cpy(WS[wi][:, 0:1024], pt[:, :], [ptk, 'ws%d' % wi], ['ws%d' % wi])
                    T.dma('sp', ld, ssmc_d[:, c * 4096 + q * 1024: c * 4096 + (q + 1) * 1024], WS[wi][:, 0:1024],
                          r=['ws%d' % wi], w=['scr_wz%d_%d' % (c, q)])
            TMPF = big(84 * KB, F32, 4096)
            a8b = lambda a: a.unsqueeze(2).to_broadcast([128, 32, 128])
            g128 = lambda a: a.rearrange("p (g m) -> p g m", m=128)
            tt(g128(TB), g128(PI_), a8b(A8i), ALU.mult, PIK + ['pwi', 'tbh0', 'tbh1'], ['tbx'])
            tt(g128(TMPF), g128(PR), a8b(A8r), ALU.mult, PRK + ['pwr'], ['tmpf'])
            tt(TMPF, TMPF, TB, ALU.subtract, ['tmpf', 'tbx'], ['tmpf'])
            cpy(BZr, TMPF, ['tmpf', 'bzr'], ['bzr'])
            tt(g128(TB), g128(PR), a8b(A8i), ALU.mult, PRK + ['pwi', 'tmpf'], ['tbx'])
            tt(g128(TMPF), g128(PI_), a8b(A8r), ALU.mult, PIK + ['pwr', 'bzr'], ['tmpf'])
            tt(TMPF, TMPF, TB, ALU.add, ['tmpf', 'tbx'], ['tmpf'])
            cpy(BZi, TMPF, ['tmpf', 'bzi'], ['bzi'])
            for c, BZ in enumerate((BZr, BZi)):
                for q in range(4):
                    pt, ptk = bankb(4 + (c * 4 + q) % 2)
                    for gi in range(8):
                        g = q * 8 + gi
                        T.op('pe', lambda e, BZ=BZ, g=g, gi=gi, pt=pt: e.transpose(
                            pt[:, gi * 128:(gi + 1) * 128], BZ[:, g * 128:(g + 1) * 128], IDB[:]),
                            r=['bzr', 'bzi', 'idb'], w=[ptk])
                    wi = (c * 4 + q) % 2
                    cpy(WS[wi][:, 0:1024], pt[:, :], [ptk, 'ws%d' % wi], ['ws%d' % wi])
                    T.dma('sp', ld, ssmc_d[:, 20480 + c * 4096 + q * 1024: 20480 + c * 4096 + (q + 1) * 1024], WS[wi][:, 0:1024],
                          r=['ws%d' % wi], w=['scr_wz2%d_%d' % (c, q)])
            cprod(CRE, CIM, (True, 8), (False, 15), rkc)
            cpy(WS[0][:], PR, PRK + ['ws0'], ['ws0']); neg(WS[1][:], PI_, PIK + ['ws1'], ['ws1'])
            cprod(KBR, KBI, (False, 7), (True, 7), rkb)
            cpy(L_r, PR, PRK, ['lr']); cpy(L_i, PI_, PIK, ['li'])
            cprod(CRE, CIM, (True, 7), (False, 7), rkc)
            R_rf, R_nif = R_r, R_ni
            R_rv = TB[:, 0:2048].bitcast(BF16); R_niv = TB[:, 2048:4096].bitcast(BF16)
            H0, H1 = slice(0, 64), slice(64, 128)
            cpy(R_rf[H0], PR[H0], PRK, ['rr0']); neg(R_nif[H0], PI_[H0], PIK, ['rni0'])
            T.op('dve', lambda e: e.memset(R_rf[H1], 0.0), w=['rr1']); T.op('dve', lambda e: e.memset(R_nif[H1], 0.0), w=['rni1'])
            cpy(R_rv[H1], PR[H1], PRK + ['tbh0', 'tbh1'], ['rv1']); neg(R_niv[H1], PI_[H1], PIK + ['tbh0', 'tbh1'], ['rnv1'])
            T.op('dve', lambda e: e.memset(R_rv[H0], 0.0), r=['tbh0', 'tbh1'], w=['rv0']); T.op('dve', lambda e: e.memset(R_niv[H0], 0.0), r=['tbh0', 'tbh1'], w=['rnv0'])
            T.dma('sp', ld, ssmc_d[:, 8192:12288], WS[0][:], r=['ws0'], w=['scr_wy0'])
            T.dma('sp', ld, ssmc_d[:, 12288:16384], WS[1][:], r=['ws1'], w=['scr_wy1'])
            ck(5)
            ck(6)
            MFv = CTb[:, O_MF:O_MF + 128]; MRv = CTb[:, O_MR:O_MR + 128]; IDv = CTb[:, O_ID:O_ID + 128]
            for g in range(32):
                pA, pAk = bank(g % 2)
                A_ = pA[:, 0:128]; B_ = pA[:, 128:256]
                gs = slice(g * 128, (g + 1) * 128)
                for (dst, Rr_, Rn_) in ((A_, R_rf, R_nif), (B_, R_rv, R_niv)):
                    T.op('pe', lambda e, dst=dst, Rr_=Rr_, gs=gs: e.matmul(dst, lhsT=L_r[:, gs], rhs=Rr_[:, gs], start=True, stop=False),
                         r=['lr', 'rr0', 'rr1', 'rv0', 'rv1'], w=[pAk])
                    T.op('pe', lambda e, dst=dst, Rn_=Rn_, gs=gs: e.matmul(dst, lhsT=L_i[:, gs], rhs=Rn_[:, gs], start=False, stop=True),
                         r=['li', 'rni0', 'rni1', 'rnv0', 'rnv1'], w=[pAk])
                t1 = SMT[0][:, 0:128]; t2 = SMT[1][:, 0:128]
                tt(t1, A_, MFv, ALU.mult, [pAk, 'ct', 'smt0'], ['smt0'])
                tt(t2, B_, MRv, ALU.mult, [pAk, 'ct', 'smt1'], ['smt1'])
                tt(t1, t1, t2, ALU.add, ['smt0', 'smt1'], ['smt0'])
                wi = (g // 8) % 2
                T.op('dve', lambda e, g=g, wi=wi, t1=t1: e.scalar_tensor_tensor(
                    out=WS[wi][:, (g % 8) * 128:(g % 8 + 1) * 128], in0=IDv, scalar=SMk[:, O_DSH + g:O_DSH + g + 1], in1=t1,
                    op0=ALU.mult, op1=ALU.add), r=['smt0', 'ct', 'smk', 'ws%d' % wi], w=['ws%d' % wi])
                if g % 8 == 7:
                    q = g // 8
                    T.dma('sp', ld, ssmc_d[:, 16384 + q * 1024:16384 + (q + 1) * 1024], WS[wi][:, 0:1024],
                          r=['ws%d' % wi], w=['scr_t%d' % q])
            ck(7)
            if debug:
                dbg['pwr'] = (PWR, [128, 512], F32); dbg['pwi'] = (PWI, [128, 512], F32)
                dbg['kbr'] = (KBR, [128, 512], F32); dbg['atab'] = (ATAB[:].rearrange("p a b -> p (a b)"), [128, 256], F32)
                dbg['s1'] = (S1[:].rearrange("p a b -> p (a b)"), [128, 40], F32)
                dbg['mods'] = (MODS[:].rearrange("p a b c -> p (a b c)"), [128, 160], F32)
                dd = nc.dram_tensor("dbg_ssmc", [128, 28672], BF16, kind="ExternalOutput").ap()
                T.dma('sp', ld, dd[:, :], ssmc_d[:, :], r=['scr_wy0', 'scr_wy1'] + ['scr_wz%d_%d' % (c, q) for c in range(2) for q in range(4)] + ['scr_t%d' % q for q in range(4)] + ['scr_wz2%d_%d' % (c, q) for c in range(2) for q in range(4)], w=['dbg_ssmc'])
                dbg_ssmc = dd
                for nm, (ap, shp, dt_) in list(dbg.items()):
                    dd = nc.dram_tensor("dbg_" + nm, shp, dt_, kind="ExternalOutput").ap()
                    T.dma('sp', ld, dd[:, :], ap, r=[], w=['dbg_' + nm])
                    dbg[nm] = dd
                dbg['ssmc'] = dbg_ssmc
            T.fence()

            xs = [newslot(), newslot(), newslot()]
            os_ = [newslot(), newslot()]
            gs_slot = newslot()
            ssmc_slot = newslot()
            def mm(out, lhsT, rhs, start, stop, r, w, **kw):
                T.op('pe', lambda e: e.matmul(out, lhsT=lhsT, rhs=rhs, start=start, stop=stop, **kw), r=r, w=w)

            def actf(out, in_, func, r, w, **kw):
                T.op('act', lambda e: e.activation(out=out, in_=in_, func=func, **kw), r=r, w=w)

            def vtt(eng, out, a, b, op, r, w):
                T.op(eng, lambda e: e.tensor_tensor(out=out, in0=a, in1=b, op=op), r=r, w=w)

            def trn(out, in_, ident, r, w):
                T.op('pe', lambda e: e.transpose(out, in_, ident), r=r, w=w)

            def norm_mod_T(src_ap, src_key, i, col0, Ssc, Bsh, jj, hkey):
                ss = STAT[:, 4 * i:4 * i + 1]; sq = STAT[:, 4 * i + 1:4 * i + 2]; rs = STAT[:, 4 * i + 2:4 * i + 3]
                sk = 'st%d' % i
                actf(XN[i][:], src_ap, AF.Square, [src_key], ['xn%d' % i, sk], accum_out=ss)
                actf(sq, ss, AF.Sqrt, [sk], [sk], scale=1.0 / D, bias=EPST[:, 0:1])
                T.op('dve', lambda e: e.reciprocal(out=rs, in_=sq), r=[sk], w=[sk])
                T.op('dve', lambda e: e.tensor_scalar(out=XN[i][:], in0=src_ap, scalar1=rs, scalar2=None, op0=ALU.mult),
                     r=[src_key, sk], w=['xn%d' % i])
                ptA, ptAk = bankb(i)
                ptB, ptBk = bankb(6 + i)
                for kt in range(8):
                    pt_, ptk_ = (ptA, ptAk) if kt < 4 else (ptB, ptBk)
                    trn(pt_[:, (kt % 4) * 128:(kt % 4 + 1) * 128], XN[i][:, kt * 128:(kt + 1) * 128], IDB[:], ['xn%d' % i], [ptk_])

                def part_b():
                    for kt in range(4):
                        actf(HT[:, kt, col0:col0 + 128], ptA[:, kt * 128:(kt + 1) * 128], AF.Identity, [ptAk], [hkey + 'e'],
                             scale=Ssc[:, kt, jj:jj + 1], bias=Bsh[:, kt, jj:jj + 1])
                    for kt in range(4, 8):
                        T.op('dve', lambda e, kt=kt: e.scalar_tensor_tensor(
                            out=HT[:, kt, col0:col0 + 128], in0=ptB[:, (kt - 4) * 128:(kt - 3) * 128], scalar=Ssc[:, kt, jj:jj + 1],
                            in1=Bsh[:, kt, jj:jj + 1].to_broadcast([128, 128]), op0=ALU.mult, op1=ALU.add),
                            r=[ptBk], w=[hkey + 'o'])
                return part_b

            bigv = lambda off, dt_, n: big(off, dt_, n)
            UTv = bigv(0, BF16, 32 * 288).rearrange("p (g n) -> p g n", n=288)
            XBv = bigv(18 * KB, BF16, 2 * 32 * 256).rearrange("p (c g k) -> p c g k", c=2, g=32)
            RGv = bigv(50 * KB, F32, 34 * 64).rearrange("p (t n) -> p t n", n=64)
            SSMCv = bigv(59 * KB, BF16, 20480)
            WZv = SSMCv[:, 0:8192].rearrange("p (c g m) -> p c g m", c=2, g=32)
            WYv = SSMCv[:, 8192:16384].rearrange("p (c g m) -> p c g m", c=2, g=32)
            TTv = SSMCv[:, 16384:20480].rearrange("p (g m) -> p g m", g=32)
            Uv = [bigv(100 * KB + u * 8 * KB, BF16, 4096).rearrange("p (s n) -> p s n", s=8) for u in range(2)]
            YAT = bigv(0, BF16, 4 * 2048).rearrange("p (c n) -> p c n", c=4)
            YBT = bigv(16 * KB, BF16, 4 * 2048).rearrange("p (c n) -> p c n", c=4)
            MT = bigv(32 * KB, BF16, 8 * 1024).rearrange("p (c n) -> p c n", c=8)
            ZT = bigv(52 * KB, BF16, 4 * 1024).rearrange("p (c n) -> p c n", c=4)
            CX = bigv(60 * KB, BF16, 2052)
            BSB = bigv(60 * KB + 4104, BF16, 2048)
            X1 = bigv(52 * KB, F32, 16 * 1024).rearrange("p (t n) -> p t n", t=16)
            ACTT = bigv(0, BF16, 4 * 2048).rearrange("p (c n) -> p c n", c=4)
            WDv = bigv(16 * KB, BF16, 4 * 1024).rearrange("p (c n) -> p c n", c=4)
            UAV = [[bigv(24 * KB + (b_ * 2 + av) * 4 * KB, BF16, 2048) for av in range(2)] for b_ in range(2)]
            DGF = [bigv(40 * KB + b_ * 4608, BF16, 18 * 128).rearrange("p (t n) -> p t n", t=18) for b_ in range(2)]
            SMTB = [SMT[i_][:, 0:256].bitcast(BF16) for i_ in range(4)]
            wsl = [[newslot() for _ in range(4)] for _ in range(2)]
            wd_slot = newslot()
            wz2_slot = newslot()
            HTK = lambda tb: 'hTb%d' % tb
            HTR = lambda tb: ['hTb%de' % tb, 'hTb%do' % tb]
            HTCR = ['hTce', 'hTco']
            wkeys = lambda wi_: ['ws%d' % wi_, 'ws%da' % wi_, 'ws%db' % wi_, 'ws%dc' % wi_]
            ws_n = [0]

            def next_ws():
                i = ws_n[0] % 2
                ws_n[0] += 1
                return i

            for j in range(NB):
                jc = 4
                pend = None
                for t_ in range(18):
                    i = t_ % 2
                    xi = t_ % 3 if j == 0 else (t_ + 1) % 3
                    src = ctx_d[j, t_ * 128:(t_ + 1) * 128, :] if t_ < 2 else x_d[j, (t_ - 2) * 128:(t_ - 1) * 128, :]
                    if j == 0 or t_ >= 2:
                        T.dma('sp', xs[xi], XT[xi][:], src, w=['xt%d' % xi])
                    if t_ == 2:
                        T.dma('sp', ssmc_slot, SSMCv, ssmc_d[:, 0:20480], w=['ssmc'])
                    hkey = 'hTc' if t_ < 2 else HTK((t_ - 2) // 4)
                    pb_ = norm_mod_T(XT[xi][:], 'xt%d' % xi, i, t_ * 128, S1, MODS[:, 0], jc if t_ < 2 else j, hkey)
                    if pend is not None:
                        pend()
                    pend = pb_
                pend()
                if debug and j == 0:
                    dump('ht', HT[:].rearrange("p k n -> p (k n)"), [128, 8 * NTOK], BF16, HTCR + [k_ for b_ in range(4) for k_ in HTR(b_)])
                ck(10)
                wi = next_ws()
                T.dma('pool', wsl[wi][0], WS[wi][:].rearrange("p (k n) -> p k n", n=512),
                      win_d[:, 0:512].rearrange("(k p) n -> p k n", p=128), w=wkeys(wi))
                WUv = WS[wi][:].rearrange("p (k n) -> p k n", n=512)
                pbk = [0]
                for bi, (M, base, cbase, hk) in enumerate(((32, 0, 0, HTCR), (128, 256, 32, HTR(0) + HTR(1)), (128, 1280, 160, HTR(2) + HTR(3)))):
                    ub = bi % 2
                    for s in range(8):
                        ps, psk = bank(2 + pbk[0] % 4); pbk[0] += 1
                        for kt in range(8):
                            mm(ps[0:M, :], HT[:, kt, base + s: base + 8 * M: 8], WUv[:, kt, :], kt == 0, kt == 7,
                               hk + ['ws%d' % wi], [psk])
                        actf(Uv[ub].rearrange("p s n -> p (s n)").rearrange("p (g s h) -> p g s h", g=32, s=8)[0:M, :, s, :],
                             ps[0:M, :].rearrange("p (g h) -> p g h", h=16), AF.Copy, [psk], ['U%d' % ub])
                    for q in range(4):
                        ptb, ptk = bankb(q % 2)
                        for gi in range(8):
                            g = q * 8 + gi
                            trn(ptb[:, gi * 128: gi * 128 + M], Uv[ub].rearrange("p s n -> p (s n)")[0:M, g * 128:(g + 1) * 128], IDB[0:M, 0:M],
                                ['U%d' % ub], [ptk])
                        T.op('dve', lambda e, ptb=ptb, q=q, M=M, cbase=cbase: e.tensor_copy(
                            out=UTv[:, q * 8:(q + 1) * 8, cbase:cbase + M],
                            in_=ptb[:, :].rearrange("p (g m) -> p g m", m=128)[:, :, 0:M]), r=[ptk], w=['UT'])
                if debug and j == 0:
                    dump('ut', UTv.rearrange("p g n -> p (g n)"), [128, 32 * 288], BF16, ['UT'])
                ck(11)
                WZ2v = bigv(100 * KB, BF16, 8192).rearrange("p (c g m) -> p c g m", c=2, g=32)
                T.dma('sp', wz2_slot, bigv(100 * KB, BF16, 8192), ssmc_d[:, 20480:28672], w=['U0', 'U1', 'wz2'])
                T.op('dve', lambda e: e.memset(RGv[:, 0:2, :], 0.0), w=['rg'])
                ACAT = ATAB[:, 0, :]; ASW = ATAB[:, 1, :]
                T1 = SMT[0][:, 0:128]; T2 = SMT[1][:, 0:128]
                c4 = lambda a: a.rearrange("p (t c g) -> p t c g", t=2, c=2)
                SL = 32
                rsl = lambda h_, l_: slice(h_, (l_ - 1 if l_ > 0 else None), -1)
                for sg in range(NCH // SL):
                    zp = PB[sg % 2][:, :].rearrange("p (c g t) -> p c g t", c=2, g=32)
                    zk = ['ps%d' % (4 * (sg % 2) + u) for u in range(4)]
                    if sg == 0:
                        hi, lo = 31, 0
                    else:
                        hi, lo = 319 - SL * sg, 288 - SL * sg
                    for c in range(2):
                        for g in range(32):
                            t0 = 1 if sg == 0 else 0
                            mm(zp[0:64, c, g, :], WZv[:, c, g, 0:64], UTv[:, g, SL * sg:SL * (sg + 1)], True, False, ['UT', 'ssmc'], zk)
                            mm(zp[0:64, c, g, t0:SL], WZ2v[:, c, g, 0:64], UTv[:, g, SL * sg - 1 + t0:SL * (sg + 1) - 1], False, True, ['UT', 'wz2'], zk)
                            mm(zp[64:128, c, g, :], WZv[:, c, g, 64:128], UTv[:, g, rsl(hi, lo)], True, False,
                               ['UT', 'ssmc'], zk, tile_position=(0, 64))
                            if sg == 0:
                                mm(zp[64:128, c, g, 1:SL], WZ2v[:, c, g, 64:128], UTv[:, g, rsl(hi, lo + 1)], False, True,
                                   ['UT', 'wz2'], zk, tile_position=(0, 64))
                            elif sg == 1:
                                mm(zp[64:128, c, g, 0:1], WZ2v[:, c, g, 64:128], UTv[:, g, 0:1], False, False,
                                   ['UT', 'wz2'], zk, tile_position=(0, 64))
                                mm(zp[64:128, c, g, 1:SL], WZ2v[:, c, g, 64:128], UTv[:, g, rsl(hi, lo + 1)], False, True,
                                   ['UT', 'wz2'], zk, tile_position=(0, 64))
                            else:
                                mm(zp[64:128, c, g, :], WZ2v[:, c, g, 64:128], UTv[:, g, rsl(hi + 1, lo + 1)], False, True,
                                   ['UT', 'wz2'], zk, tile_position=(0, 64))
                    R = RGv
                    if sg > 0:
                        T.op('dve', lambda e: e.tensor_copy(out=RGv[:, 0:2, :], in_=RGv[:, SL:SL + 2, :]), r=['rg'], w=['rg'])
                    for m_ in range(SL // 2):
                        t_ = 2 * m_
                        prev = R[:, t_:t_ + 2, :]
                        cur = R[:, t_ + 2:t_ + 4, :]
                        p4 = prev.rearrange("p t (c g) -> p t c g", c=2)
                        vtt('dve', c4(T1), p4, c4(ACAT), ALU.mult, ['rg'], ['sT1'])
                        vtt('dve', c4(T2), p4[:, :, ::-1, :], c4(ASW), ALU.mult, ['rg'], ['sT2'])
                        vtt('dve', c4(T1), c4(T1), zp[:, :, :, t_:t_ + 2].rearrange("p c g t -> p t c g"), ALU.add, ['sT1'] + zk, ['sT1'])
                        vtt('dve', cur.rearrange("p t (c g) -> p t c g", c=2), c4(T1), c4(T2), ALU.add, ['sT1', 'sT2'], ['rg'])
                    i0 = max(SL * sg, 31); i1 = min(SL * sg + SL - 1, 286)
                    if i0 <= i1:
                        n = i1 - i0 + 1; t0 = i0 - SL * sg
                        srcv = lambda hs: R[hs, 2 + t0:2 + t0 + n, :].rearrange("p t (c g) -> p c g t", c=2)
                        actf(XBv[0:64, :, :, i0 - 31:i0 - 31 + n], srcv(slice(0, 64)), AF.Copy, ['rg'], ['XB'])
                        kh, kl = 286 - i0, 286 - i1
                        actf(XBv[64:128, :, :, kh:(kl - 1 if kl > 0 else None):-1], srcv(slice(64, 128)), AF.Copy, ['rg'], ['XB'])
                if debug and j == 0:
                    dump('xb', XBv.rearrange("p c g k -> p (c g k)"), [128, 2 * 32 * 256], BF16, ['XB'])
                ck(12)
                for blk in range(2):
                    for q in range(8):
                        yq, yk = bank(4 + q % 4)
                        for gi in range(4):
                            g = 4 * q + gi
                            o_ = yq[:, gi * 128:(gi + 1) * 128]
                            mm(o_, UTv[:, g, 32 + blk * 128:32 + (blk + 1) * 128], TTv[:, g, :], True, False, ['UT', 'ssmc'], [yk])
                            mm(o_, XBv[:, 0, g, blk * 128:(blk + 1) * 128], WYv[:, 0, g, :], False, False, ['XB', 'ssmc'], [yk])
                            mm(o_, XBv[:, 1, g, blk * 128:(blk + 1) * 128], WYv[:, 1, g, :], False, True, ['XB', 'ssmc'], [yk])
                        actf(Uv[blk][:, :, q * 64:(q + 1) * 64].rearrange("p s (g h) -> p g s h", g=4),
                             yq[:, :].rearrange("p (g s h) -> p g s h", g=4, s=8), AF.Gelu_apprx_tanh, [yk], ['ZU%d' % blk])
                if debug and j == 0:
                    dump('zu', Uv[0].rearrange("p s n -> p (s n)"), [128, 4096], BF16, ['ZU0'])
                ck(13)
                T.fence()
                wi = next_ws()
                T.dma('pool', wsl[wi][0], WS[wi][:, 0:2048].rearrange("p (k n) -> p k n", n=512),
                      gluw_d[:, :].rearrange("(k p) n -> p k n", p=128), w=wkeys(wi))
                GWv = WS[wi][:, 0:2048].rearrange("p (k n) -> p k n", n=512)
                pbk = [0]
                for blk in range(2):
                    for ct in range(4):
                        ptb, ptk = bankb(ct % 2)
                        for s in range(8):
                            trn(ptb[:, s * 128:(s + 1) * 128], Uv[blk][:, s, ct * 128:(ct + 1) * 128], IDB[:], ['ZU%d' % blk], [ptk])
                        T.op('dve', lambda e, ptb=ptb, ct=ct: e.tensor_copy(
                            out=ZT[:, ct, :].rearrange("p (k s) -> p s k", s=8),
                            in_=ptb[:, :].rearrange("p (s k) -> p s k", s=8)), r=[ptk], w=['zT'])
                    for co in range(4):
                        for tb2 in range(2):
                            ps, psk = bank(2 + pbk[0] % 4); pbk[0] += 1
                            sgi = pbk[0] % 2
                            for ci in range(4):
                                mm(ps, GWv[:, ci, co * 128:(co + 1) * 128], ZT[:, ci, tb2 * 512:(tb2 + 1) * 512], ci == 0, ci == 3,
                                   ['zT', 'ws%d' % wi], [psk])
                            actf(SMTB[sgi], ps, AF.Sigmoid, [psk], ['smt%d' % sgi], bias=SMk[:, O_GLUB + co:O_GLUB + co + 1])
                            vtt('dve', YAT[:, co, blk * 1024 + tb2 * 512: blk * 1024 + (tb2 + 1) * 512],
                                ZT[:, co, tb2 * 512:(tb2 + 1) * 512], SMTB[sgi], ALU.mult, ['zT', 'smt%d' % sgi], ['yaT'])
                if debug and j == 0:
                    dump('yat', YAT.rearrange("p c n -> p (c n)"), [128, 8192], BF16, ['yaT'])
                ck(14)
                T.op('dve', lambda e: e.memset(CX[:, 0:1], 0.0), w=['cx'])
                T.op('dve', lambda e: e.memset(CX[:, 2049:2050], 0.0), w=['cx'])
                w3 = win_d[:, 512:2048].rearrange("(k p) (jj c n) -> p k jj c n", p=128, jj=3, c=4)
                pbk = [0]
                for ct in range(4):
                    wi = next_ws()
                    Wv = WS[wi][:, 0:3072].rearrange("p (k jj n) -> p k jj n", k=8, jj=3)
                    wk3 = ['ws%d' % wi, 'ws%da' % wi, 'ws%db' % wi]
                    for jj in range(3):
                        c0_ = 512 + jj * 512 + ct * 128
                        T.dma('pool', wsl[wi][jj], Wv[:, :, jj, :], win_d[:, c0_:c0_ + 128].rearrange("(k p) n -> p k n", p=128),
                              w=(wkeys(wi) if jj == 0 else [wk3[jj]]))
                    for tb in range(4):
                        cols = slice(256 + tb * 512, 256 + (tb + 1) * 512)
                        bks = [bank(2 + (pbk[0] + u) % 6) for u in range(3)]; pbk[0] += 3
                        for jj in range(3):
                            for kt in range(8):
                                mm(bks[jj][0], Wv[:, kt, jj, :], HT[:, kt, cols], kt == 0, kt == 7, HTR(tb) + [wk3[jj]], [bks[jj][1]])
                        actf(SMT[2][:, :], bks[1][0], AF.Copy, [bks[1][1]], ['smt2'])
                        vtt('dve', CX[:, 1 + tb * 512:1 + (tb + 1) * 512], SMT[2][:, :], bks[2][0], ALU.mult, ['smt2', bks[2][1]], ['cx'])
                        actf(BSB[:, tb * 512:(tb + 1) * 512], bks[0][0], AF.Copy, [bks[0][1]], ['bsb'])
                    for tb in range(4):
                        cps, cpk = bank(2 + pbk[0] % 6); pbk[0] += 1
                        for k in range(3):
                            mm(cps, DGS[:, ct * 3 + k, :], CX[:, tb * 512 + k: tb * 512 + k + 512], k == 0, k == 2, ['cx'], [cpk])
                        vtt('dve', YBT[:, ct, tb * 512:(tb + 1) * 512], BSB[:, tb * 512:(tb + 1) * 512], cps, ALU.mult, ['bsb', cpk], ['ybT'])
                if debug and j == 0:
                    dump('ybt', YBT.rearrange("p c n -> p (c n)"), [128, 8192], BF16, ['ybT'])
                ck(15)
                wg = win_d[:, 2048:4096].rearrange("(k p) (jj t n) -> p k jj t n", p=128, jj=2, t=8)
                for blk in range(2):
                    for jt in range(8):
                        wi = next_ws()
                        Wg = WS[wi][:, 0:2048].rearrange("p (k jj n) -> p k jj n", k=8, jj=2)
                        PAv = WS[wi][:, 2048:2560].rearrange("p (k n) -> p k n", k=4)
                        PBv = WS[wi][:, 2560:3072].rearrange("p (k n) -> p k n", k=4)
                        wk = 'ws%d' % wi
                        T.dma('pool', wsl[wi][0], Wg[:, :, 0, :], win_d[:, 2048 + jt * 128:2048 + (jt + 1) * 128].rearrange("(k p) n -> p k n", p=128), w=wkeys(wi))
                        T.dma('pool', wsl[wi][3], Wg[:, :, 1, :], win_d[:, 3072 + jt * 128:3072 + (jt + 1) * 128].rearrange("(k p) n -> p k n", p=128), w=[wk + 'c'])
                        T.dma('pool', wsl[wi][1], PAv, pa_d[:, jt * 128:(jt + 1) * 128].rearrange("(k p) n -> p k n", p=128), w=[wk + 'a'])
                        T.dma('pool', wsl[wi][2], PBv, pb_d[:, jt * 128:(jt + 1) * 128].rearrange("(k p) n -> p k n", p=128), w=[wk + 'b'])
                        for tb2 in range(2):
                            tok = slice(blk * 1024 + tb2 * 512, blk * 1024 + (tb2 + 1) * 512)
                            tbk = HTR(blk * 2 + tb2)
                            cols = slice(256 + tok.start, 256 + tok.stop)
                            b0 = 4 * ((jt * 2 + tb2) % 2)
                            (GA, gak), (GB, gbk), (AA, aak), (BB, bbk) = [bank(b0 + u) for u in range(4)]
                            for kt in range(8):
                                mm(GA, Wg[:, kt, 0, :], HT[:, kt, cols], kt == 0, kt == 7, tbk + [wk], [gak])
                            for kt in range(8):
                                mm(GB, Wg[:, kt, 1, :], HT[:, kt, cols], kt == 0, kt == 7, tbk + [wk + 'c'], [gbk])
                            for ci in range(4):
                                mm(AA, PAv[:, ci, :], YAT[:, ci, tok], ci == 0, ci == 3, ['yaT', wk + 'a'], [aak])
                            for ci in range(4):
                                mm(BB, PBv[:, ci, :], YBT[:, ci, tok], ci == 0, ci == 3, ['ybT', wk + 'b'], [bbk])
                            actf(SMT[0][:, :], GA, AF.Sigmoid, [gak], ['smt0'])
                            actf(SMT[1][:, :], GB, AF.Sigmoid, [gbk], ['smt1'])
                            vtt('dve', SMT[2][:, :], SMT[0][:, :], AA, ALU.mult, ['smt0', aak], ['smt2'])
                            vtt('dve', SMT[3][:, :], SMT[1][:, :], BB, ALU.mult, ['smt1', bbk], ['smt3'])
                            vtt('dve', MT[:, jt, tb2 * 512:(tb2 + 1) * 512], SMT[2][:, :], SMT[3][:, :], ALU.add, ['smt2', 'smt3'], ['mT'])
                    if debug and j == 0 and blk == 0:
                        dump('mt', MT.rearrange("p c n -> p (c n)"), [128, 8192], BF16, ['mT'])
                    T.dma('sp', gs_slot, GT[:], gsc_d[j * 2 + 0, :, :], w=['gt'])
                    for half in range(2):
                        T.dma('pool', wsl[half][0], WS[half][:].rearrange("p (k n) -> p k n", n=512),
                              wo_d[:, half * 512:(half + 1) * 512].rearrange("(k p) n -> p k n", p=128), w=wkeys(half))
                        T.op('dve', lambda e, half=half: e.tensor_tensor(
                            out=WS[half][:].rearrange("p (k n) -> p k n", n=512), in0=WS[half][:].rearrange("p (k n) -> p k n", n=512),
                            in1=GT[:, half * 512:(half + 1) * 512].unsqueeze(1).to_broadcast([128, 8, 512]), op=ALU.mult),
                            r=['gt'], w=['ws%d' % half])
                    ws_n[0] = 0
                    pbk = [0]
                    pend = None

                    def stage_w(tt_):
                        tile = blk * 8 + tt_
                        xi = tt_ % 3
                        T.dma('sp', xs[xi], XT[xi][:], x_d[j, tile * 128:(tile + 1) * 128, :], w=['xt%d' % xi])
                        for half in range(2):
                            ps, psk = bank(2 + pbk[0] % 4); pbk[0] += 1
                            WOv = WS[half][:].rearrange("p (k n) -> p k n", n=512)
                            for kt in range(8):
                                mm(ps, MT[:, kt, tt_ * 128:(tt_ + 1) * 128], WOv[:, kt, :], kt == 0, kt == 7, ['mT', 'ws%d' % half], [psk])
                            vtt('dve', X1[:, tile, half * 512:(half + 1) * 512], XT[xi][:, half * 512:(half + 1) * 512], ps, ALU.add,
                                ['xt%d' % xi, psk], ['x1_%d' % tile])

                    def stage_n(tt_):
                        tile = blk * 8 + tt_
                        return norm_mod_T(X1[:, tile, :], 'x1_%d' % tile, tt_ % 2, 256 + tile * 128, S2, MODS[:, 2], j, HTK(tile // 4))
                    for tt_ in range(10):
                        if tt_ < 8:
                            stage_w(tt_)
                        if tt_ >= 2:
                            pb_ = stage_n(tt_ - 2)
                            if pend is not None:
                                pend()
                            pend = pb_
                    pend()
                if debug and j == 0:
                    dump('x1', X1.rearrange("p t n -> p (t n)"), [128, 16384], F32, ['x1_%d' % t_ for t_ in range(16)])
                    dump('h2t', HT[:].rearrange("p k n -> p (k n)"), [128, 8 * NTOK], BF16, HTCR + [k_ for b_ in range(4) for k_ in HTR(b_)])
                ck(16)
                Wup0 = WS[0][:, 0:2048].rearrange("p (k jj n) -> p k jj n", k=8, jj=2)
                for av in range(2):
                    T.dma('pool', wsl[0][av], Wup0[:, :, av, :], wup_d[:, av * FFH:av * FFH + 128].rearrange("(k p) n -> p k n", p=128),
                          w=(wkeys(0) if av == 0 else ['ws0a']))
                T.fence()
                T.dma('sp', gs_slot, GT[:], gsc_d[j * 2 + 1, :, :], w=['gt'])
                wu = wup_d[:, :].rearrange("(k p) (jj m) -> p k jj m", p=128, jj=2)
                gsz = [4, 4, 4, 4, 3, 3]
                taps = [(1, 1)] + [(a_, b_) for a_ in range(3) for b_ in range(3) if (a_, b_) != (1, 1)]
                DVE_TAPS = []
                i_pair = 0
                pb_up = [0]; pb_cv = [0]
                for gi_, n_g in enumerate(gsz):
                    i0 = i_pair
                    for li in range(n_g):
                        ip = i0 + li
                        bf = ip % 2
                        wi = next_ws()
                        wk = 'ws%d' % wi
                        Wup = WS[wi][:, 0:2048].rearrange("p (k jj n) -> p k jj n", k=8, jj=2)
                        assert ip > 0 or wi == 0
                        for av in range(2):
                            if ip == 0:
                                continue
                            c0_ = av * FFH + ip * 128
                            T.dma('pool', wsl[wi][av], Wup[:, :, av, :], wup_d[:, c0_:c0_ + 128].rearrange("(k p) n -> p k n", p=128),
                                  w=(wkeys(wi) if av == 0 else [wk + 'a']))
                        if li == 0:
                            T.dma('pool', wd_slot, WDv[:, 0:n_g, :], wdn_d[i0 * 128:(i0 + n_g) * 128, :].rearrange("(k p) n -> p k n", p=128), w=['wd'])
                        def build_dgf(ipp, part=None):
                            bfp = ipp % 2
                            lst = [(av_, tp) for av_ in range(2) for tp in range(9) if tp != 4]
                            if part is not None:
                                lst = lst[part::4]
                            for (av_, tp) in lst:
                                tl_ = ipp + 22 * av_
                                T.op('dve', lambda e, bfp=bfp, av_=av_, tp=tp, tl_=tl_: e.tensor_scalar(
                                    out=DGF[bfp][:, av_ * 9 + tp, :], in0=IDB[:], scalar1=SMk[:, O_CONVW + tl_ * 9 + tp:O_CONVW + tl_ * 9 + tp + 1],
                                    scalar2=None, op0=ALU.mult), r=[], w=['dgf%d' % bfp])
                        if ip == 0:
                            build_dgf(0)
                        for av in range(2):
                            for tb in range(4):
                                ps, psk = bank(pb_up[0] % 4); pb_up[0] += 1
                                for kt in range(8):
                                    mm(ps, Wup[:, kt, av, :], HT[:, kt, 256 + tb * 512:256 + (tb + 1) * 512], kt == 0, kt == 7,
                                       HTR(tb) + [wk if av == 0 else wk + 'a'], [psk])
                                actf(UAV[bf][av][:, tb * 512:(tb + 1) * 512], ps, AF.Copy, [psk], ['uav%d%d' % (bf, av)])
                        for tb in range(4):
                            cvb = []
                            for av in range(2):
                                cps, cpk = bank(4 + pb_cv[0] % 4); pb_cv[0] += 1
                                cvb.append((cps, cpk))
                                src3 = UAV[bf][av].rearrange("p (r c) -> p r c", c=64)
                                dst3 = cps.rearrange("p (r c) -> p r c", c=64)
                                tl = ip + 22 * av
                                first = (2, 1) if tb < 3 else (0, 1)
                                pe_taps = [first] + [t_ for t_ in taps if t_ not in ((1, 1), first)]
                                for ti_, (ta, tb_) in enumerate(pe_taps):
                                    dr, dc = ta - 1, tb_ - 1
                                    r0 = max(8 * tb, -dr); r1 = min(8 * tb + 8, 32 - dr)
                                    c0 = max(0, -dc); c1 = min(64, 64 - dc)
                                    if ti_ == 0:
                                        assert (r0, r1, c0, c1) == (8 * tb, 8 * tb + 8, 0, 64)
                                    mm(dst3[:, r0 - 8 * tb:r1 - 8 * tb, c0:c1], DGF[bf][:, av * 9 + ta * 3 + tb_, :],
                                       src3[:, r0 + dr:r1 + dr, c0 + dc:c1 + dc], ti_ == 0, ti_ == len(pe_taps) - 1,
                                       ['uav%d%d' % (bf, av), 'dgf%d' % bf], [cpk])
                            sgi = tb % 2
                            wc = lambda av_: SMk[:, O_CONVW + (ip + 22 * av_) * 9 + 4:O_CONVW + (ip + 22 * av_) * 9 + 5]
                            ta_ = SMT[2 * sgi][:, :]; tv_ = SMT[2 * sgi + 1][:, :]
                            sa_ = XN[sgi][:, 0:512]
                            T.op('dve', lambda e, ta_=ta_, c_=cvb[0][0], u_=UAV[bf][0][:, tb * 512:(tb + 1) * 512], w_=wc(0): e.scalar_tensor_tensor(
                                out=ta_, in0=u_, scalar=w_, in1=c_, op0=ALU.mult, op1=ALU.add),
                                r=['uav%d0' % bf, cvb[0][1]], w=['smt%d' % (2 * sgi)])
                            actf(sa_, ta_, AF.Silu, ['smt%d' % (2 * sgi)], ['xn%d' % sgi])
                            T.op('dve', lambda e, tv_=tv_, c_=cvb[1][0], u_=UAV[bf][1][:, tb * 512:(tb + 1) * 512], w_=wc(1): e.scalar_tensor_tensor(
                                out=tv_, in0=u_, scalar=w_, in1=c_, op0=ALU.mult, op1=ALU.add),
                                r=['uav%d1' % bf, cvb[1][1]], w=['smt%d' % (2 * sgi + 1)])
                            vtt('dve', ACTT[:, li, tb * 512:(tb + 1) * 512], sa_, tv_, ALU.mult, ['xn%d' % sgi, 'smt%d' % (2 * sgi + 1)], ['actT'])
                            if ip + 1 < NPAIR:
                                build_dgf(ip + 1, part=tb)
                        if li == 0:
                            T.op('dve', lambda e, n_g=n_g: e.tensor_tensor(
                                out=WDv[:, 0:n_g, :], in0=WDv[:, 0:n_g, :], in1=GT[:].unsqueeze(1).to_broadcast([128, n_g, 1024]), op=ALU.mult),
                                r=['gt'], w=['wd'])
                    i_pair += n_g
                    last = gi_ == len(gsz) - 1
                    for tt_ in range(16):
                        i = tt_ % 2
                        for half in range(2):
                            ps, psk = bank(pb_up[0] % 4); pb_up[0] += 1
                            for li in range(n_g):
                                mm(ps, ACTT[:, li, tt_ * 128:(tt_ + 1) * 128], WDv[:, li, half * 512:(half + 1) * 512], li == 0, li == n_g - 1,
                                   ['actT', 'wd'], [psk])
                            vtt('dve', X1[:, tt_, half * 512:(half + 1) * 512], X1[:, tt_, half * 512:(half + 1) * 512], ps, ALU.add,
                                [psk], ['x1_%d' % tt_])
                        if last:
                            ss = STAT[:, 4 * i:4 * i + 1]; sq = STAT[:, 4 * i + 1:4 * i + 2]; rs = STAT[:, 4 * i + 2:4 * i + 3]
                            sk = 'st%d' % i
                            actf(XN[i][:], X1[:, tt_, :], AF.Square, ['x1_%d' % tt_], ['xn%d' % i, sk], accum_out=ss)
                            actf(sq, ss, AF.Sqrt, [sk], [sk], scale=1.0 / D, bias=EPST[:, 0:1])
                            T.op('dve', lambda e, rs=rs, sq=sq: e.reciprocal(out=rs, in_=sq), r=[sk], w=[sk])
                            eb, ebk = ((XT[0], 'xt0'), (GT, 'gt'))[i]
                            T.op('dve', lambda e, eb=eb, tt_=tt_, rs=rs: e.scalar_tensor_tensor(
                                out=eb[:], in0=X1[:, tt_, :], scalar=rs, in1=FG[:], op0=ALU.mult, op1=ALU.mult),
                                r=['x1_%d' % tt_, sk], w=[ebk])
                            T.dma('sp', os_[i], out_d[j, tt_ * 128:(tt_ + 1) * 128, :], eb[:], r=[ebk], w=['out%d_%d' % (j, tt_)])
                            if tt_ == 0 and j + 1 < NB:
                                for t2_ in range(2):
                                    T.dma('sp', xs[1 + t2_], XT[1 + t2_][:], ctx_d[j + 1, t2_ * 128:(t2_ + 1) * 128, :], w=['xt%d' % (1 + t2_)])
                ck(17)
                T.fence()

        except StopBuild:
            pass
        T.fence()
        with nc.Block() as block:
            @block.tensor
            def _(e):
                T.emit('pe', e, sems)

            @block.scalar
            def _(e):
                T.emit('act', e, sems)

            @block.vector
            def _(e):
                T.emit('dve', e, sems)

            @block.gpsimd
            def _(e):
                T.emit('pool', e, sems)

            @block.sync
            def _(e):
                T.emit('sp', e, sems)
    return nc, dbg


def prep_core(inp, ci, NB):
    f = np.float32
    b0 = ci * NB
    d = {}
    d["x"] = np.ascontiguousarray(inp["x"][b0:b0 + NB])
    d["ctx"] = np.ascontiguousarray(inp["ctx"][b0:b0 + NB])
    cc = np.concatenate([inp["c"][b0:b0 + NB], np.broadcast_to(inp["c_ctx"][None, :], (5 - NB, D))], 0)
    d["cT"] = np.ascontiguousarray(cc.reshape(5, 8, 128).transpose(2, 1, 0)).astype(f)
    d["mod_w"] = np.ascontiguousarray(inp["mod_w"][0])
    d["modb_row"] = np.ascontiguousarray(inp["mod_b"][0][None, :])
    d["fg_row"] = np.ascontiguousarray(inp["final_g"][None, :])
    sm = np.zeros((128, NSM), f)
    fm = lambda v, nt: np.ascontiguousarray(v.reshape(nt, 128).T)
    sm[:, O_N1G:O_N1G + 8] = fm(inp["norm1_g"][0], 8)
    sm[:, O_N2G:O_N2G + 8] = fm(inp["norm2_g"][0], 8)
    sm[:, O_GLUB:O_GLUB + 4] = fm(inp["ssm_glu_b"][0], 4)
    sm[:, O_SCONV:O_SCONV + 12] = inp["sconv_w"][0].reshape(3, 4, 128).transpose(2, 1, 0).reshape(128, 12)
    sm[:, O_CONVW:O_CONVW + 396] = inp["ffn_conv_w"][0].reshape(9, 44, 128).transpose(2, 1, 0).reshape(128, 396)
    sm[:, O_MODB:O_MODB + 48] = fm(inp["mod_b"][0], 48)
    sm[:, O_DSH:O_DSH + 32] = np.tile(inp["ssm_d"][0].reshape(32, 16).T, (8, 1))
    sm[:, O_LRE:O_LRE + 32] = inp["ssm_lambda_re"][0].transpose(0, 2, 1).reshape(128, 32)
    sm[:, O_LIM:O_LIM + 32] = inp["ssm_lambda_im"][0].transpose(0, 2, 1).reshape(128, 32)
    sm[:, O_LDT:O_LDT + 32] = np.repeat(inp["ssm_log_dt"][0][:, None, :], 64, 1).reshape(128, 32)
    d["smalls"] = sm
    ct = np.zeros((128, NCT), f)
    ct[:, O_ID:O_ID + 128] = np.eye(128, dtype=f)
    s_idx = np.arange(128) // 16
    ct[:, O_MF:O_MF + 128] = (s_idx[None, :] >= s_idx[:, None]).astype(f)
    ct[:, O_MR:O_MR + 128] = (s_idx[None, :] <= s_idx[:, None]).astype(f)
    ct[:, O_NV:O_NV + 512] = np.repeat(np.arange(-7, 9, dtype=f), 32)[None, :]
    d["consts"] = ct
    bc = np.zeros((128, 4, 32, 16), f)
    bc[:, 0] = inp["ssm_b_re"][0].transpose(0, 2, 1, 3).reshape(128, 32, 16)
    bc[:, 1] = inp["ssm_b_im"][0].transpose(0, 2, 1, 3).reshape(128, 32, 16)
    bc[:, 2] = inp["ssm_c_re"][0].transpose(0, 3, 1, 2).reshape(128, 32, 16)
    bc[:, 3] = inp["ssm_c_im"][0].transpose(0, 3, 1, 2).reshape(128, 32, 16)
    d["ssmbc"] = bc
    d["w_in"] = np.ascontiguousarray(inp["w_in"][0])
    d["glu_w"] = np.ascontiguousarray(inp["ssm_glu_w"][0])
    d["proj_a"] = np.ascontiguousarray(inp["proj_a"][0])
    d["proj_b"] = np.ascontiguousarray(inp["proj_b"][0])
    d["w_out"] = np.ascontiguousarray(inp["w_out"][0])
    d["w_up"] = np.ascontiguousarray(inp["ffn_w_up"][0])
    d["w_down"] = np.ascontiguousarray(inp["ffn_w_down"][0])
    return d


_CACHE = {}


def kernel(**inputs):
    inp = {k: np.asarray(v, dtype=np.float32) for k, v in inputs.items()}
    NB, NCORE = 4, 8
    if "prog" not in _CACHE:
        _CACHE["prog"] = build_program(NB)[0]
    nc = _CACHE["prog"]
    in_maps = [prep_core(inp, ci, NB) for ci in range(NCORE)]
    res = run_bass_kernel_spmd(nc, in_maps, core_ids=list(range(NCORE)))
    return np.concatenate([r["out"] for r in res.results], axis=0).astype(np.float32)
```
